# Optimizing a Trainium2 kernel written in Bass

```python
import math
import jax, jax.numpy as jnp
from jax import lax
import numpy as np

D_MODEL = 1024
BATCH = 16
SEQ = 2048
DEPTH = 2

N_MIXERS = 2
EXPAND = 2
D_INNER = EXPAND * D_MODEL
POOL_WINDOWS = (2, 4, 8, 16)
N_POOL_GROUPS = len(POOL_WINDOWS)
POOL_GROUP_DIM = D_INNER // N_POOL_GROUPS
SB_HEADS = 16
SB_QK_DIM = 64
SB_V_DIM = D_INNER // SB_HEADS
SB_QK_WIDTH = SB_HEADS * SB_QK_DIM
Q_BLOCK = 128
RMS_EPS = 1e-6
N_POOL_LAYERS = (DEPTH + 1) // 2
N_SB_LAYERS = DEPTH // 2

kernel_name = "hybrid_pool_stickbreaking_trunk"


def rms_norm(x, g):
    x32 = x.astype(jnp.float32)
    inv = lax.rsqrt(jnp.mean(x32 * x32, axis=-1, keepdims=True) + RMS_EPS)
    return (x32 * inv * g.astype(jnp.float32)).astype(x.dtype)


def causal_pool_minus_self(u, window):
    S = u.shape[1]
    u32 = u.astype(jnp.float32)
    c = jnp.cumsum(u32, axis=1)
    c_shift = jnp.pad(c, ((0, 0), (window, 0), (0, 0)))[:, :S]
    count = jnp.minimum(jnp.arange(S) + 1, window).astype(jnp.float32)
    mean = (c - c_shift) / count[None, :, None]
    return (mean - u32).astype(u.dtype)


def pooling_mixer(u, w_in, w_group, scale, w_out):
    B, S, _ = u.shape
    proj = u @ w_in
    xb, z = jnp.split(proj, 2, axis=-1)
    xg = xb.reshape(B, S, N_POOL_GROUPS, POOL_GROUP_DIM)
    pooled = jnp.stack(
        [causal_pool_minus_self(xg[:, :, gi], w) for gi, w in enumerate(POOL_WINDOWS)],
        axis=2)
    mixed = jnp.einsum('bsgi,gio->bsgo', pooled, w_group).reshape(B, S, D_INNER)
    y = mixed * scale * jax.nn.silu(z)
    return y @ w_out


def stick_breaking_attention(q, k, v):
    S = q.shape[2]
    scale = 1.0 / math.sqrt(q.shape[-1])
    n_blocks = S // Q_BLOCK
    outs = []
    for bi in range(n_blocks):
        q_lo = bi * Q_BLOCK
        k_len = q_lo + Q_BLOCK
        qb = q[:, :, q_lo:q_lo + Q_BLOCK].astype(jnp.float32)
        kb = k[:, :, :k_len].astype(jnp.float32)
        vb = v[:, :, :k_len].astype(jnp.float32)
        z = jnp.einsum('bhqd,bhkd->bhqk', qb, kb) * scale
        q_pos = q_lo + jnp.arange(Q_BLOCK)
        k_pos = jnp.arange(k_len)
        mask = k_pos[None, :] < q_pos[:, None]
        log_beta = jax.nn.log_sigmoid(z)
        log_om = jnp.where(mask, jax.nn.log_sigmoid(-z), 0.0)
        later = lax.cumsum(log_om, axis=3, reverse=True) - log_om
        a = jnp.where(mask, jnp.exp(log_beta + later), 0.0)
        outs.append(jnp.einsum('bhqk,bhkd->bhqd', a, vb))
    return jnp.concatenate(outs, axis=2).astype(v.dtype)


def stick_breaking_mixer(u, w_in, w_out):
    B, S, _ = u.shape
    proj = u @ w_in
    q, k, v, z = jnp.split(
        proj, [SB_QK_WIDTH, 2 * SB_QK_WIDTH, 2 * SB_QK_WIDTH + D_INNER], axis=-1)
    q = q.reshape(B, S, SB_HEADS, SB_QK_DIM).transpose(0, 2, 1, 3)
    k = k.reshape(B, S, SB_HEADS, SB_QK_DIM).transpose(0, 2, 1, 3)
    v = v.reshape(B, S, SB_HEADS, SB_V_DIM).transpose(0, 2, 1, 3)
    o = stick_breaking_attention(q, k, v)
    o = o.transpose(0, 2, 1, 3).reshape(B, S, D_INNER)
    y = o * jax.nn.silu(z)
    return y @ w_out


def setup_inputs(seed: int = 0) -> dict:
    key = jax.random.key(seed)
    ks = jax.random.split(key, 10)
    f32 = jnp.float32
    x = jax.random.normal(ks[0], (BATCH, SEQ, D_MODEL), f32)
    norm_g = 1.0 + 0.02 * jax.random.normal(ks[1], (DEPTH, D_MODEL), f32)
    pool_w_in = jax.random.normal(ks[2], (N_POOL_LAYERS, D_MODEL, 2 * D_INNER), f32) * D_MODEL ** -0.5
    pool_w = jax.random.normal(ks[3], (N_POOL_LAYERS, N_POOL_GROUPS, POOL_GROUP_DIM, POOL_GROUP_DIM), f32) * POOL_GROUP_DIM ** -0.5
    pool_scale = 1.0 + 0.02 * jax.random.normal(ks[4], (N_POOL_LAYERS, D_INNER), f32)
    pool_w_out = jax.random.normal(ks[5], (N_POOL_LAYERS, D_INNER, D_MODEL), f32) * D_INNER ** -0.5
    sb_w_in = jax.random.normal(ks[6], (N_SB_LAYERS, D_MODEL, 2 * SB_QK_WIDTH + 2 * D_INNER), f32) * D_MODEL ** -0.5
    sb_w_out = jax.random.normal(ks[7], (N_SB_LAYERS, D_INNER, D_MODEL), f32) * D_INNER ** -0.5
    norm_f = 1.0 + 0.02 * jax.random.normal(ks[8], (D_MODEL,), f32)
    return {"x": x, "norm_g": norm_g, "pool_w_in": pool_w_in, "pool_w": pool_w,
            "pool_scale": pool_scale, "pool_w_out": pool_w_out, "sb_w_in": sb_w_in,
            "sb_w_out": sb_w_out, "norm_f": norm_f}


def reference(x, norm_g, pool_w_in, pool_w, pool_scale, pool_w_out, sb_w_in, sb_w_out, norm_f):
    h = x
    for i in range(DEPTH):
        u = rms_norm(h, norm_g[i])
        j = i // N_MIXERS
        if i % N_MIXERS == 0:
            h = h + pooling_mixer(u, pool_w_in[j], pool_w[j], pool_scale[j], pool_w_out[j])
        else:
            h = h + stick_breaking_mixer(u, sb_w_in[j], sb_w_out[j])
    return rms_norm(h, norm_f)
```

```python
import os
import numpy as np
import concourse.bass as bass
import concourse.mybir as mybir
from concourse.bass_utils import run_bass_kernel_spmd

F32 = mybir.dt.float32
BF16 = mybir.dt.bfloat16
U8 = mybir.dt.uint8
ALU = mybir.AluOpType
AF = mybir.ActivationFunctionType

NCORES = 8
SEQ = 2048
D = 1024
NT = SEQ // 128
EPS = 1e-6
POOL_W = (2, 4, 8, 16)
COMPUTE = ("pe", "act", "dve", "pool")


class Buf:
    __slots__ = ("name", "start", "size", "lw", "rd", "al")

    def __init__(self, name, start=None, size=0):
        self.name, self.start, self.size = name, start, size
        self.lw = {}
        self.rd = {}
        self.al = [self]


class Sched:
    def __init__(self):
        self.prog = {e: [] for e in COMPUTE + ("sp",)}
        self.cnt = {e: 0 for e in COMPUTE}
        self.seen = {e: {} for e in COMPUTE + ("sp",)}
        self.dcnt = {}

    def _waits(self, eng, reads, writes):
        waits = {}
        seen = self.seen[eng]

        def need(k, v, raw):
            if k == eng and (eng == "pe" or not raw):
                return
            if seen.get(k, 0) >= v:
                return
            if waits.get(k, 0) < v:
                waits[k] = v

        for b in reads:
            for a in b.al:
                for k, v in a.lw.items():
                    need(k, v, True)
        for b in writes:
            for a in b.al:
                for k, v in a.lw.items():
                    need(k, v, False)
                for k, v in a.rd.items():
                    need(k, v, False)
        for k, v in waits.items():
            seen[k] = v
        return waits

    def op(self, eng, emit, reads=(), writes=(), sig=True):
        waits = self._waits(eng, reads, writes)
        if sig:
            self.cnt[eng] += 1
            val = self.cnt[eng]
        else:
            val = self.cnt[eng] + 1
        self.prog[eng].append((waits, emit, eng if sig else None))
        for b in reads:
            b.rd[eng] = val
        for b in writes:
            b.lw[eng] = val

    def dma(self, q, emit, semkey, reads=(), writes=()):
        waits = self._waits(q, reads, writes)
        self.dcnt[semkey] = self.dcnt.get(semkey, 0) + 16
        val = self.dcnt[semkey]
        self.prog[q].append((waits, emit, semkey))
        for b in reads:
            b.rd[semkey] = val
        for b in writes:
            b.lw[semkey] = val


def build_program(mode="full"):
    do_l0 = mode in ("full", "l0")
    do_l1 = mode in ("full", "l1")
    nc = bass.Bass("TRN2", target_bir_lowering=False)
    S = Sched()

    def din(name, shape):
        return nc.dram_tensor(name, shape, F32, kind="ExternalInput").ap()

    x_d = din("x", [2, SEQ, D])
    out_d = nc.dram_tensor("out", [2, SEQ, D], F32, kind="ExternalOutput").ap()
    g0b_d, g1b_d, gfb_d = din("g0b", [128, D]), din("g1b", [128, D]), din("gfb", [128, D])
    psc_d = din("pscale", [128, 16])
    w0in_d = din("w0in", [8, 128, 8 * 512])
    w0g_d = din("w0g", [4, 128, 4 * 512])
    w0o_d = din("w0o", [4, 128, 4 * 1024])
    w1in_d = din("w1in", [8, 128, 8 * 768])
    w1o_d = din("w1o", [4, 128, 4 * 1024])

    bufs = []
    pos = [0]

    def region(name, size, at=None):
        if at is None:
            at = pos[0]
            pos[0] = at + ((size + 31) // 32) * 32
        b = Buf(name, at, size)
        bufs.append(b)
        return b

    hB = [region(f"h{t}", D * 4) for t in range(NT)]
    uTB = [region(f"uT{j}", 8 * 512 * 2) for j in range(4)]
    gbB = region("gb", D * 4)
    yTB = [region(f"yT{j}", 2048 * 2) for j in range(4)]
    woB = [region("wo0", 4 * 1024 * 2)]
    uB = [region(f"u{j}", D * 2) for j in range(2)]
    identB = region("ident", 128 * 2)
    zerosB = region("zeros", 8 * 4)
    rcB = region("rc16", 16 * 4)
    pscB = region("psc", 16 * 4)
    msqB = region("msq", 16 * 4)
    rstdB = region("rstd", 16 * 4)
    phase0 = pos[0]
    xbTB = region("xbT", 4 * 1040 * 4)
    tmpB_ = [region(f"tmp{j}", 1040 * 4) for j in range(2)]
    winB = [region(f"win{j}", 8 * 512 * 2) for j in range(3)]
    wgB = [region(f"wg{j}", 4 * 512 * 2) for j in range(2)]
    haloB = region("halo", 16 * 16 * 4)
    fixB = region("fix", 16 * 4)
    end_l0 = pos[0]
    pooledB = region("pooled", 4 * 1024 * 2, at=uTB[2].start)
    sz0B = [region(f"sz0{j}", 1024 * 4, at=uTB[3].start + j * 4096) for j in range(2)]
    pos[0] = phase0
    wslB = region("wslab", 8 * 768 * 2)
    qTB = region("qT", 2048 * 2)
    kTB = [region(f"kT{j}", 2048 * 2) for j in range(2)]
    vSB = [region(f"vS{j}", 16 * 128 * 2) for j in range(2)]
    szTB = [region(f"szT{j}", 1024 * 4) for j in range(2)]
    DEPTH = int(os.environ.get("KDEPTH", "5"))
    NOM, NR, NA = DEPTH + 1, 3, 3
    omB = [region(f"om{j}", 513 * 4) for j in range(NOM)]
    RB = [region(f"R{j}", 513 * 4) for j in range(NR)]
    AB = [region(f"A{j}", 512 * 2) for j in range(NA)]
    ATB = [region(f"AT{j}", 16 * 256 * 2) for j in range(2)]
    end_l1 = pos[0]
    total = max(end_l0, end_l1)
    for i, a in enumerate(bufs):
        for b in bufs[i + 1:]:
            if a.start < b.start + b.size and b.start < a.start + a.size:
                a.al.append(b)
                b.al.append(a)

    arena = nc.alloc_sbuf_tensor("arena", [128, total], U8)

    def view(b, dt, pat=None, parts=128, **kw):
        v = arena[0:parts, b.start:b.start + b.size].bitcast(dt)
        if pat:
            v = v.rearrange(pat, **kw)
        return v

    hV = [view(b, F32) for b in hB]
    h4V = [arena[:, hB[4 * j].start:hB[4 * j].start + 4 * D * 4].bitcast(F32)
           .rearrange("p (t d) -> p t d", t=4) for j in range(4)]
    uTV = [view(b, BF16, "p (k n) -> p k n", k=8) for b in uTB]
    gbV = view(gbB, F32)
    yTV = [view(b, BF16) for b in yTB]
    yT0V = [arena[:, yTB[2 * j].start:yTB[2 * j].start + 8192].bitcast(BF16)
            .rearrange("p (k n) -> p k n", k=4) for j in range(2)]
    yT0B = [[yTB[0], yTB[1]], [yTB[2], yTB[3]]]
    woV = [view(b, BF16, "p (k n) -> p k n", k=4) for b in woB]
    uV = [view(b, BF16) for b in uB]
    junkB, junkV = uB[0], uV[0]
    identV = view(identB, BF16)
    zerosV = view(zerosB, F32)
    rcV, pscV, msqV, rstdV = view(rcB, F32), view(pscB, F32), view(msqB, F32), view(rstdB, F32)
    xbTV = view(xbTB, F32, "p (j n) -> p j n", j=4)
    tmpV = [view(b, F32) for b in tmpB_]
    winV = [view(b, BF16, "p (k n) -> p k n", k=8) for b in winB]
    wgV = [view(b, BF16, "p (k n) -> p k n", k=4) for b in wgB]
    haloV = view(haloB, F32, "p (c n) -> p c n", c=16)
    fixV = view(fixB, F32)
    pooledV = view(pooledB, BF16, "p (k n) -> p k n", k=4)
    sz0V = [view(b, F32) for b in sz0B]
    wslV = view(wslB, BF16, "p (k n) -> p k n", k=8)
    qTV = view(qTB, BF16)
    kTV = [view(b, BF16) for b in kTB]
    vSV = [view(b, BF16, "p (t n) -> p t n", t=16) for b in vSB]
    szTV = [view(b, F32) for b in szTB]
    omV = [view(b, F32) for b in omB]
    RV = [view(b, F32) for b in RB]
    AV_ = [view(b, BF16) for b in AB]
    ATV = [view(b, BF16, "p (c n) -> p c n", c=16) for b in ATB]

    psum = nc.alloc_psum_tensor("psum", [128, 8 * 512], F32)
    psum_bf = psum[:, :].bitcast(BF16)
    bankB = [Buf(f"bank{i}") for i in range(8)]
    bptr = [0]

    NBANK = 6

    def alloc_banks(n):
        while n == 2 and (bptr[0] % 2 or bptr[0] + 1 >= NBANK):
            bptr[0] = (bptr[0] + 1) % NBANK
        b0 = bptr[0]
        bptr[0] = (b0 + n) % NBANK
        return b0

    pzB = Buf("pzero")
    oB = [Buf("o_half0"), Buf("o_half1")]
    oV = [psum[:, 6 * 512 + hf * 256:6 * 512 + (hf + 1) * 256] for hf in range(2)]
    pzV = psum[:, 7 * 512:7 * 512 + 1]

    regcache = {}

    def preg(e, val):
        if val not in regcache:
            regcache[val] = e.to_reg(val)
        return regcache[val]

    def mm(out, lhsT, rhs, start, stop, reads, writes, sig):
        S.op("pe", lambda e: e.matmul(out, lhsT, rhs, start=start, stop=stop), reads, writes, sig)

    def act(out, in_, func, reads, writes, **kw):
        S.op("act", lambda e: e.activation(out, in_, func, **kw), reads, writes)

    def dma_w(out, in_, semkey, writes):
        S.dma("pool", lambda e: e.dma_start(out=out, in_=in_), semkey, (), writes)

    def dma_sp(out, in_, semkey, reads=(), writes=()):
        S.dma("sp", lambda e: e.dma_start(out=out, in_=in_), semkey, reads, writes)

    S.op("pool", lambda e: e.memset(junkV[:, 0:128], 1.0), (), [junkB])
    S.op("pool", lambda e: e.affine_select(identV, junkV[:, 0:128], [[-1, 128]], ALU.is_equal, preg(e, 0.0),
                                           base=0, channel_multiplier=1), [junkB], [identB])
    S.op("pool", lambda e: e.memset(zerosV, 0.0), (), [zerosB])
    S.op("dve", lambda e: e.tensor_copy(psum[:, 7 * 512:7 * 512 + 8], zerosV[:, 0:8]), [zerosB], [pzB])
    for t in range(16):
        S.op("pool", (lambda t: lambda e: e.memset(rcV[:, t:t + 1], 1.0 / (t + 1)))(t), (), [rcB])
    dma_sp(pscV, psc_d[:, :], "c_psc", (), [pscB])

    def rms_stats(tiles):
        t0, t1 = tiles[0], tiles[-1] + 1
        for t in tiles:
            act(junkV, hV[t], AF.Square, [hB[t]], [junkB, msqB], scale=1.0 / 32.0, accum_out=msqV[:, t:t + 1])
        S.op("dve", lambda e: e.tensor_scalar(rstdV[:, t0:t1], msqV[:, t0:t1], EPS, None, ALU.add), [msqB], [rstdB])
        S.op("act", lambda e: e.sqrt(rstdV[:, t0:t1], rstdV[:, t0:t1]), [rstdB], [rstdB])
        S.op("dve", lambda e: e.reciprocal(rstdV[:, t0:t1], rstdV[:, t0:t1]), [rstdB], [rstdB])

    def rmsnorm_tile(t, out_ap, out_bufs):
        S.op("dve", lambda e: e.scalar_tensor_tensor(out_ap, hV[t], rstdV[:, t:t + 1], gbV, ALU.mult, ALU.mult),
             [hB[t], rstdB, gbB], out_bufs)

    def make_uT(t, blk, coff):
        us = t % 2
        rmsnorm_tile(t, uV[us], [uB[us]])
        b = alloc_banks(1)
        for k in range(8):
            S.op("pe", (lambda k: lambda e: e.transpose(psum_bf[:, b * 1024 + k * 128: b * 1024 + (k + 1) * 128],
                                                        uV[us][:, k * 128:(k + 1) * 128], identV))(k),
                 [uB[us], identB], [bankB[b]], sig=(k == 7))
        act(uTV[blk][:, :, coff:coff + 128],
            psum_bf[:, b * 1024:(b + 1) * 1024].rearrange("p (k n) -> p k n", k=8),
            AF.Copy, [bankB[b]], [uTB[blk]])

    def load_x(s, j):
        src = x_d[s, 512 * j:512 * (j + 1), :].rearrange("(t p) d -> p t d", p=128)
        dma_sp(h4V[j], src, f"hx{j}", (), hB[4 * j:4 * j + 4])

    def store_h(s, t):
        dma_sp(out_d[s, 128 * t:128 * (t + 1), :], hV[t], f"hs{t // 4}", [hB[t]], ())

    def out_proj(yk_list, yk_bufs, wo_slot, tiles, tok0):
        nk = len(yk_list)
        for t in tiles:
            for n in range(2):
                b = alloc_banks(1)
                po = psum[:, b * 512:(b + 1) * 512]
                for k in range(nk):
                    c0 = (t - tok0) * 128
                    mm(po, yk_list[k][:, c0:c0 + 128], woV[wo_slot][:, k, n * 512:(n + 1) * 512],
                       k == 0, k == nk - 1, yk_bufs + [woB[wo_slot]], [bankB[b]], k == nk - 1)
                hv = hV[t][:, n * 512:(n + 1) * 512]
                S.op("dve", (lambda hv, po: lambda e: e.tensor_tensor(hv, hv, po, ALU.add))(hv, po),
                     [bankB[b], hB[t]], [hB[t]])

    for s in range(2):
        for j in range(4):
            load_x(s, j)

        if do_l0:
            dma_sp(gbV, g0b_d[:, :], "c_gb", (), [gbB])
            win_i = [0]
            pending = None
            for hs in range(2):
                tiles = list(range(hs * 8, hs * 8 + 8))
                rms_stats(tiles)
                for t in tiles:
                    make_uT(t, (t - hs * 8) // 4, ((t - hs * 8) % 4) * 128)
                for g in range(4):
                    w = POOL_W[g]
                    sx, sz = win_i[0] % 3, (win_i[0] + 1) % 3
                    win_i[0] += 2
                    gs = g % 2
                    dma_w(winV[sx], w0in_d[g, :, :].rearrange("p (k n) -> p k n", k=8), f"win{sx}", [winB[sx]])
                    dma_w(winV[sz], w0in_d[4 + g, :, :].rearrange("p (k n) -> p k n", k=8), f"win{sz}", [winB[sz]])
                    dma_w(wgV[gs], w0g_d[g, :, :].rearrange("p (k n) -> p k n", k=4), f"wg{gs}", [wgB[gs]])
                    if hs == 0:
                        S.op("dve", lambda e: e.memset(xbTV[:, :, 0:16], 0.0), (), [xbTB])
                    else:
                        S.op("dve", (lambda g: lambda e: e.tensor_copy(xbTV[:, :, 0:16], haloV[:, 4 * g:4 * g + 4, :]))(g),
                             [haloB], [xbTB])
                    for j in range(4):
                        b = alloc_banks(2)
                        for n in range(2):
                            for k in range(8):
                                mm(psum[:, (b + n) * 512:(b + n + 1) * 512], winV[sx][:, k, j * 128:(j + 1) * 128],
                                   uTV[n][:, k, :], k == 0, k == 7, [winB[sx], uTB[n]], [bankB[b + n]], k == 7)
                        act(xbTV[:, j, 16:1040], psum[:, b * 512:(b + 2) * 512], AF.Copy,
                            [bankB[b], bankB[b + 1]], [xbTB])
                    if pending is not None:
                        pending()
                        pending = None
                    dma_w(woV[0], w0o_d[g, :, :].rearrange("p (k n) -> p k n", k=4), "wo0", [woB[0]])
                    for j in range(4):
                        X = xbTV[:, j, :]
                        src, lo = X, 0
                        step, ti = 1, 0
                        while step < w:
                            dst = tmpV[ti]
                            nlo = lo + step
                            S.op("dve", (lambda dst, src, nlo, step: lambda e: e.tensor_tensor(
                                dst[:, nlo:1040], src[:, nlo:1040], src[:, nlo - step:1040 - step], ALU.add))(dst, src, nlo, step),
                                 [xbTB, tmpB_[1 - ti]] if src is not X else [xbTB], [tmpB_[ti]])
                            src, lo = dst, nlo
                            step *= 2
                            ti = 1 - ti
                        last = 1 - ti
                        S.op("dve", (lambda src, X, j, w: lambda e: e.scalar_tensor_tensor(
                            pooledV[:, j, :], src[:, 16:1040], 1.0 / w, X[:, 16:1040], ALU.mult, ALU.subtract))(src, X, j, w),
                             [tmpB_[last], xbTB], [pooledB])
                        if hs == 0:
                            S.op("dve", (lambda src, w: lambda e: e.tensor_tensor(
                                fixV[:, 0:w - 1], src[:, 16:16 + w - 1], rcV[:, 0:w - 1], ALU.mult))(src, w),
                                 [tmpB_[last], rcB], [fixB])
                            S.op("dve", (lambda X, j, w: lambda e: e.tensor_tensor(
                                pooledV[:, j, 0:w - 1], fixV[:, 0:w - 1], X[:, 16:16 + w - 1], ALU.subtract))(X, j, w),
                                 [fixB, xbTB], [pooledB])
                        if hs == 0:
                            S.op("dve", (lambda X, g, j: lambda e: e.tensor_copy(haloV[:, 4 * g + j, :], X[:, 1024:1040]))(X, g, j),
                                 [xbTB], [haloB])
                    for j in range(4):
                        zs = j % 2
                        b = alloc_banks(2)
                        for n in range(2):
                            for k in range(8):
                                mm(psum[:, (b + n) * 512:(b + n + 1) * 512], winV[sz][:, k, j * 128:(j + 1) * 128],
                                   uTV[n][:, k, :], k == 0, k == 7, [winB[sz], uTB[n]], [bankB[b + n]], k == 7)
                        act(sz0V[zs], psum[:, b * 512:(b + 2) * 512], AF.Silu, [bankB[b], bankB[b + 1]], [sz0B[zs]])
                        b2 = alloc_banks(2)
                        for n in range(2):
                            for k in range(4):
                                mm(psum[:, (b2 + n) * 512:(b2 + n + 1) * 512], wgV[gs][:, k, j * 128:(j + 1) * 128],
                                   pooledV[:, k, n * 512:(n + 1) * 512], k == 0, k == 3, [wgB[gs], pooledB],
                                   [bankB[b2 + n]], k == 3)
                        c = 4 * g + j
                        S.op("dve", (lambda b2, c, j, zs, gs: lambda e: e.scalar_tensor_tensor(
                            yT0V[gs][:, j, :], psum[:, b2 * 512:(b2 + 2) * 512], pscV[:, c:c + 1], sz0V[zs],
                            ALU.mult, ALU.mult))(b2, c, j, zs, gs),
                             [bankB[b2], bankB[b2 + 1], pscB, sz0B[zs]], yT0B[gs])
                    pending = (lambda gs, tiles, hs: lambda: out_proj(
                        [yT0V[gs][:, k, :] for k in range(4)], yT0B[gs], 0, tiles, hs * 8))(gs, tiles, hs)
                pending()
                pending = None

        if mode == "l0":
            for t in range(NT):
                store_h(s, t)
            continue

        dma_sp(gbV, g1b_d[:, :], "c_gb", (), [gbB])
        rms_stats(list(range(NT)))
        for t in range(NT):
            make_uT(t, t // 4, (t % 4) * 128)
        for j in range(NOM):
            S.op("pool", (lambda j: lambda e: e.memset(omV[j][:, 512:513], 1.0))(j), (), [omB[j]])
        S.op("pool", lambda e: e.memset(kTV[0][64:128, :], 0.0), (), [kTB[0]])
        S.op("pool", lambda e: e.memset(kTV[1][0:64, :], 0.0), (), [kTB[1]])

        seg_i = [0]

        def proj_qk(hp):
            for half in range(2):
                for which in ("q", "k"):
                    col0 = 0 if which == "q" else 128
                    b = alloc_banks(2)
                    for n in range(2):
                        blk = half * 2 + n
                        for k in range(8):
                            mm(psum[:, (b + n) * 512:(b + n + 1) * 512], wslV[:, k, col0:col0 + 128],
                               uTV[blk][:, k, :], k == 0, k == 7, [wslB, uTB[blk]], [bankB[b + n]], k == 7)
                    cs = slice(half * 1024, (half + 1) * 1024)
                    if which == "q":
                        act(qTV[:, cs], psum[:, b * 512:(b + 2) * 512], AF.Copy,
                            [bankB[b], bankB[b + 1]], [qTB], scale=0.125)
                    else:
                        act(kTV[0][0:64, cs], psum[0:64, b * 512:(b + 2) * 512], AF.Copy,
                            [bankB[b], bankB[b + 1]], [kTB[0]])
                        act(kTV[1][64:128, cs], psum[64:128, b * 512:(b + 2) * 512], AF.Copy,
                            [bankB[b], bankB[b + 1]], [kTB[1]])

        def proj_z(head):
            hh = head % 2
            for half in range(2):
                b = alloc_banks(2)
                for n in range(2):
                    blk = half * 2 + n
                    for k in range(8):
                        mm(psum[:, (b + n) * 512:(b + n + 1) * 512], wslV[:, k, 512 + hh * 128:512 + (hh + 1) * 128],
                           uTV[blk][:, k, :], k == 0, k == 7, [wslB, uTB[blk]], [bankB[b + n]], k == 7)
                act(szTV[half], psum[:, b * 512:(b + 2) * 512], AF.Silu, [bankB[b], bankB[b + 1]], [szTB[half]])

        def v_half_unit(head, tq, part):
            hh, vs = head % 2, head % 2
            b = alloc_banks(1)
            for j, tt in enumerate((2 * part, 2 * part + 1)):
                for k in range(8):
                    mm(psum[:, b * 512 + j * 128:b * 512 + (j + 1) * 128],
                       uTV[tq][:, k, tt * 128:(tt + 1) * 128], wslV[:, k, 256 + hh * 128:256 + (hh + 1) * 128],
                       k == 0, k == 7, [wslB, uTB[tq]], [bankB[b]], (k == 7 and j == 1))
            t0 = tq * 4 + 2 * part
            act(vSV[vs][:, t0:t0 + 2, :],
                psum[:, b * 512:b * 512 + 256].rearrange("p (t n) -> p t n", t=2), AF.Copy,
                [bankB[b]], [vSB[vs]])

        for head in range(16):
            if True:
                hp, hh = divmod(head, 2)
                grp, hin = divmod(head, 4)
                vSc = head % 2
                if head == 0:
                    dma_w(wslV, w1in_d[0, :, :].rearrange("p (k n) -> p k n", k=8), "wsl", [wslB])
                    proj_qk(0)
                    for tq in range(4):
                        v_half_unit(0, tq, 0)
                        v_half_unit(0, tq, 1)
                if hin == 0:
                    dma_w(woV[0], w1o_d[grp, :, :].rearrange("p (k n) -> p k n", k=4), "wo0", [woB[0]])
                filler = [(head + 1, tq, part) for tq in range(4) for part in range(2)] if head < 15 else []

                segs = []
                for sb2 in range(8):
                    for i in (2 * sb2, 2 * sb2 + 1):
                        nseg = i // 4 + 1
                        for sg in reversed(range(nseg)):
                            diag = (sg == nseg - 1)
                            w = ((i % 4) + 1) * 128 if diag else 512
                            segs.append(dict(sb2=sb2, i=i, sg=sg, w=w, diag=diag, hin=hin, hh=hh,
                                             last=(sg == 0 and i == 2 * sb2 + 1)))
                prev = None
                for sd in segs:
                    sd["n"] = seg_i[0]
                    seg_i[0] += 1
                    sd["prev"] = None if sd["diag"] else prev
                    prev = sd

                def stage_a(sd):
                    i, sg, w, sl = sd["i"], sd["sg"], sd["w"], sd["n"] % NOM
                    b = alloc_banks(1)
                    zp = psum[:, b * 512:b * 512 + w]
                    if os.environ.get("KZPAD", "1") == "1":
                        mm(zp, qTV[:, i * 128:(i + 1) * 128], kTV[sd["hh"]][:, sg * 512:sg * 512 + w], True, True,
                           [qTB, kTB[sd["hh"]]], [bankB[b]], True)
                    else:
                        p0 = sd["hh"] * 64
                        mm(zp, qTV[p0:p0 + 64, i * 128:(i + 1) * 128], kTV[sd["hh"]][p0:p0 + 64, sg * 512:sg * 512 + w],
                           True, True, [qTB, kTB[sd["hh"]]], [bankB[b]], True)
                    act(omV[sl][:, 512 - w:512], zp, AF.Sigmoid, [bankB[b]], [omB[sl]], scale=-1.0)

                def stage_m(sd):
                    if sd["diag"]:
                        sl = sd["n"] % NOM
                        S.op("pool", lambda e: e.affine_select(omV[sl][:, 384:512], omV[sl][:, 384:512], [[-1, 128]],
                                                               ALU.is_gt, preg(e, 1.0), base=0, channel_multiplier=1),
                             [omB[sl]], [omB[sl]])

                def stage_b(sd):
                    i, sg, w = sd["i"], sd["sg"], sd["w"]
                    so, sr, sa = sd["n"] % NOM, sd["n"] % NR, sd["n"] % NA
                    lo = 512 - w
                    if sd["prev"] is None:
                        init, rds = 1.0, [omB[so], pzB]
                    else:
                        p = sd["prev"]
                        plo, pr = 512 - p["w"], p["n"] % NR
                        init, rds = RV[pr][:, plo:plo + 1], [omB[so], pzB, RB[pr]]
                    S.op("dve", lambda e: e.tensor_tensor_scan(RV[sr][:, lo:513][:, ::-1], omV[so][:, lo:513][:, ::-1],
                                                               pzV.broadcast_to([128, w + 1]), init, ALU.mult, ALU.add),
                         rds, [RB[sr]])
                    S.op("pool", lambda e: e.tensor_tensor(AV_[sa][:, 0:w], RV[sr][:, lo + 1:513], RV[sr][:, lo:512],
                                                           ALU.subtract), [RB[sr]], [AB[sa]])
                    sl = sa
                    nb = w // 128
                    b = alloc_banks(1)
                    for blk in range(nb):
                        S.op("pe", (lambda blk: lambda e: e.transpose(
                            psum_bf[:, b * 1024 + blk * 128:b * 1024 + (blk + 1) * 128],
                            AV_[sl][:, blk * 128:(blk + 1) * 128], identV))(blk),
                             [AB[sl], identB], [bankB[b]], sig=(blk == nb - 1))
                    ats = sd["sb2"] % 2
                    qo = (i % 2) * 128
                    act(ATV[ats][:, 4 * sg:4 * sg + nb, qo:qo + 128],
                        psum_bf[:, b * 1024:b * 1024 + w].rearrange("p (c n) -> p c n", c=nb), AF.Copy,
                        [bankB[b]], [ATB[ats]])

                def stage_c(sd):
                    sb2 = sd["sb2"]
                    ats, hf = sb2 % 2, sb2 % 2
                    nch = 2 * sb2 + 2
                    for c in range(nch):
                        lo2 = 128 if c == nch - 1 else 0
                        mm(oV[hf][:, lo2:256], vSV[vSc][:, c, :], ATV[ats][:, c, lo2:256],
                           c == 0, c == nch - 1, [vSB[vSc], ATB[ats]], [oB[hf]], c == nch - 1)

                def stage_d(sd):
                    sb2 = sd["sb2"]
                    hf = sb2 % 2
                    half, co = sb2 // 4, (sb2 % 4) * 256
                    y_out = yTV[sd["hin"]][:, sb2 * 256:(sb2 + 1) * 256]
                    S.op("dve", lambda e: e.tensor_tensor(y_out, oV[hf], szTV[half][:, co:co + 256], ALU.mult),
                         [oB[hf], szTB[half]], [yTB[sd["hin"]]])

                last_of = {sd["sb2"]: k for k, sd in enumerate(segs) if sd["last"]}
                c_at = {}
                for sb2, k in last_of.items():
                    c_at.setdefault(min(k + (2 if sb2 == 0 else 3), len(segs) - 1), []).append(sb2)
                done_c, done_d = set(), set()

                def emit_c(sb2):
                    if sb2 >= 2 and (sb2 - 2) not in done_d:
                        stage_d(segs[last_of[sb2 - 2]])
                        done_d.add(sb2 - 2)
                    stage_c(segs[last_of[sb2]])
                    done_c.add(sb2)

                for idx in range(len(segs) + DEPTH):
                    if idx < len(segs):
                        stage_a(segs[idx])
                    if idx >= DEPTH:
                        k = idx - DEPTH
                        stage_b(segs[k])
                        for sb2 in c_at.get(k, []):
                            emit_c(sb2)
                    if idx < len(segs):
                        stage_m(segs[idx])
                    if idx == 2:
                        proj_z(head)
                        if hh == 1 and hp < 7:
                            dma_w(wslV, w1in_d[hp + 1, :, :].rearrange("p (k n) -> p k n", k=8), "wsl", [wslB])
                    if idx == 40 and hh == 1 and hp < 7:
                        proj_qk(hp + 1)
                    if filler and idx >= (4 if hh == 0 else 16) and idx % 4 == 0:
                        v_half_unit(*filler.pop(0))
                while filler:
                    v_half_unit(*filler.pop(0))
                for sb2 in range(8):
                    if sb2 not in done_c:
                        emit_c(sb2)
                for sb2 in range(8):
                    if sb2 not in done_d:
                        stage_d(segs[last_of[sb2]])
                        done_d.add(sb2)

                if hin == 3:
                    out_proj([yTV[k] for k in range(4)], list(yTB), 0, list(range(NT)), 0)

        dma_sp(gbV, gfb_d[:, :], "c_gb", (), [gbB])
        rms_stats(list(range(NT)))
        for t in range(NT):
            rmsnorm_tile(t, hV[t], [hB[t]])
            store_h(s, t)

    sems = {}
    for k in list(COMPUTE) + list(S.dcnt.keys()):
        sems[k] = nc.alloc_semaphore(name=f"s_{k}")
    fin = {k: v for k, v in S.dcnt.items() if k.startswith("hs")}
    S.prog["sp"].append((fin, None, None))

    def replay(eng_name, e):
        for waits, emit, sig in S.prog[eng_name]:
            for k, v in waits.items():
                e.wait_ge(sems[k], v)
            if emit is None:
                continue
            ins = emit(e)
            if sig is not None:
                ins.then_inc(sems[sig], 1 if sig in COMPUTE else 16)

    with nc.Block() as block:
        @block.sync
        def _(e):
            replay("sp", e)

        @block.gpsimd
        def _(e):
            replay("pool", e)

        @block.scalar
        def _(e):
            replay("act", e)

        @block.vector
        def _(e):
            replay("dve", e)

        @block.tensor
        def _(e):
            replay("pe", e)
    return nc


_PROGS = {}


def _prog(mode):
    if mode not in _PROGS:
        _PROGS[mode] = build_program(mode)
    return _PROGS[mode]


def _layout_weights(norm_g, pool_w_in, pool_w, pool_scale, pool_w_out, sb_w_in, sb_w_out, norm_f):
    f = np.float32
    c = np.ascontiguousarray
    w0 = np.asarray(pool_w_in, f)[0]
    w0in = c(w0.reshape(8, 128, 8, 512).transpose(2, 1, 0, 3)).reshape(8, 128, 8 * 512)
    wg = np.asarray(pool_w, f)[0]
    w0g = c(wg.reshape(4, 4, 128, 512).transpose(0, 2, 1, 3)).reshape(4, 128, 4 * 512)
    wo0 = np.asarray(pool_w_out, f)[0]
    w0o = c(wo0.reshape(4, 4, 128, 1024).transpose(0, 2, 1, 3)).reshape(4, 128, 4 * 1024)
    w1 = np.asarray(sb_w_in, f)[0]
    q = w1[:, 0:1024].reshape(8, 128, 8, 128)
    kk = w1[:, 1024:2048].reshape(8, 128, 8, 128)
    v = w1[:, 2048:4096].reshape(8, 128, 8, 256)
    z = w1[:, 4096:6144].reshape(8, 128, 8, 256)
    slab = np.concatenate([q, kk, v, z], axis=3)
    w1in = c(slab.transpose(2, 1, 0, 3)).reshape(8, 128, 8 * 768)
    wo1 = np.asarray(sb_w_out, f)[0]
    w1o = c(wo1.reshape(4, 4, 128, 1024).transpose(0, 2, 1, 3)).reshape(4, 128, 4 * 1024)
    ng = np.asarray(norm_g, f)
    return {
        "g0b": c(np.broadcast_to(ng[0], (128, D))), "g1b": c(np.broadcast_to(ng[1], (128, D))),
        "gfb": c(np.broadcast_to(np.asarray(norm_f, f), (128, D))),
        "pscale": c(np.asarray(pool_scale, f)[0].reshape(16, 128).T),
        "w0in": w0in, "w0g": w0g, "w0o": w0o, "w1in": w1in, "w1o": w1o,
    }


FUSED = True


def kernel(x, norm_g, pool_w_in, pool_w, pool_scale, pool_w_out, sb_w_in, sb_w_out, norm_f):
    x = np.ascontiguousarray(np.asarray(x, np.float32))
    wmap = _layout_weights(norm_g, pool_w_in, pool_w, pool_scale, pool_w_out, sb_w_in, sb_w_out, norm_f)
    cores = list(range(NCORES))
    if FUSED:
        in_maps = [dict(wmap, x=x[2 * c:2 * c + 2]) for c in cores]
        res = run_bass_kernel_spmd(_prog("full"), in_maps, core_ids=cores)
        return np.concatenate([r["out"] for r in res.results], axis=0)
    in_maps = [dict(wmap, x=x[2 * c:2 * c + 2]) for c in cores]
    res = run_bass_kernel_spmd(_prog("l0"), in_maps, core_ids=cores)
    in_maps = [dict(wmap, x=np.ascontiguousarray(r["out"])) for r in res.results]
    res = run_bass_kernel_spmd(_prog("l1"), in_maps, core_ids=cores)
    return np.concatenate([r["out"] for r in res.results], axis=0)
```

```python
import os
import numpy as np
import concourse.bass as bass
import concourse.mybir as mybir
from concourse.bass_utils import run_bass_kernel_spmd

F32 = mybir.dt.float32
BF16 = mybir.dt.bfloat16
U8 = mybir.dt.uint8
ALU = mybir.AluOpType
AF = mybir.ActivationFunctionType

NCORES = 8
SEQ = 2048
D = 1024
NT = SEQ // 128
EPS = 1e-6
POOL_W = (2, 4, 8, 16)
COMPUTE = ("pe", "act", "dve", "pool")


class Buf:
    __slots__ = ("name", "start", "size", "lw", "rd", "al")

    def __init__(self, name, start=None, size=0):
        self.name, self.start, self.size = name, start, size
        self.lw = {}
        self.rd = {}
        self.al = [self]


class Sched:
    def __init__(self):
        self.prog = {e: [] for e in COMPUTE + ("sp",)}
        self.cnt = {e: 0 for e in COMPUTE}
        self.seen = {e: {} for e in COMPUTE + ("sp",)}
        self.dcnt = {}

    def _waits(self, eng, reads, writes):
        waits = {}
        seen = self.seen[eng]

        def need(k, v, raw):
            if k == eng and (eng == "pe" or not raw):
                return
            if seen.get(k, 0) >= v:
                return
            if waits.get(k, 0) < v:
                waits[k] = v

        for b in reads:
            for a in b.al:
                for k, v in a.lw.items():
                    need(k, v, True)
        for b in writes:
            for a in b.al:
                for k, v in a.lw.items():
                    need(k, v, False)
                for k, v in a.rd.items():
                    need(k, v, False)
        for k, v in waits.items():
            seen[k] = v
        return waits

    def op(self, eng, emit, reads=(), writes=(), sig=True):
        waits = self._waits(eng, reads, writes)
        if sig:
            self.cnt[eng] += 1
            val = self.cnt[eng]
        else:
            val = self.cnt[eng] + 1
        self.prog[eng].append((waits, emit, eng if sig else None))
        for b in reads:
            b.rd[eng] = val
        for b in writes:
            b.lw[eng] = val

    def dma(self, q, emit, semkey, reads=(), writes=()):
        waits = self._waits(q, reads, writes)
        self.dcnt[semkey] = self.dcnt.get(semkey, 0) + 16
        val = self.dcnt[semkey]
        self.prog[q].append((waits, emit, semkey))
        for b in reads:
            b.rd[semkey] = val
        for b in writes:
            b.lw[semkey] = val


def build_program(mode="full"):
    do_l0 = mode in ("full", "l0")
    do_l1 = mode in ("full", "l1")
    nc = bass.Bass("TRN2", target_bir_lowering=False)
    S = Sched()

    def din(name, shape):
        return nc.dram_tensor(name, shape, F32, kind="ExternalInput").ap()

    x_d = din("x", [2, SEQ, D])
    out_d = nc.dram_tensor("out", [2, SEQ, D], F32, kind="ExternalOutput").ap()
    g0b_d, g1b_d, gfb_d = din("g0b", [128, D]), din("g1b", [128, D]), din("gfb", [128, D])
    psc_d = din("pscale", [128, 16])
    w0in_d = din("w0in", [8, 128, 8 * 512])
    w0g_d = din("w0g", [4, 128, 4 * 512])
    w0o_d = din("w0o", [4, 128, 4 * 1024])
    w1in_d = din("w1in", [8, 128, 8 * 768])
    w1o_d = din("w1o", [4, 128, 4 * 1024])

    bufs = []
    pos = [0]

    def region(name, size, at=None):
        if at is None:
            at = pos[0]
            pos[0] = at + ((size + 31) // 32) * 32
        b = Buf(name, at, size)
        bufs.append(b)
        return b

    hB = [region(f"h{t}", D * 4) for t in range(NT)]
    uTB = [region(f"uT{j}", 8 * 512 * 2) for j in range(4)]
    gbB = region("gb", D * 4)
    uB = [region(f"u{j}", D * 2) for j in range(2)]
    yTB = [region(f"yT{j}", 2048 * 2) for j in range(4)]
    woB = [region("wo0", 4 * 1024 * 2)]
    identB = region("ident", 128 * 2)
    zerosB = region("zeros", 8 * 4)
    rcB = region("rc16", 16 * 4)
    pscB = region("psc", 16 * 4)
    msqB = region("msq", 16 * 4)
    rstdB = region("rstd", 16 * 4)
    phase0 = pos[0]
    xbTB = region("xbT", 4 * 1040 * 4)
    tmpB_ = [region(f"tmp{j}", 1040 * 4) for j in range(2)]
    winB = [region(f"win{j}", 8 * 512 * 2) for j in range(3)]
    wgB = [region(f"wg{j}", 4 * 512 * 2) for j in range(2)]
    haloB = region("halo", 16 * 16 * 4)
    fixB = region("fix", 16 * 4)
    end_l0 = pos[0]
    pooledB = region("pooled", 4 * 1024 * 2, at=uTB[2].start)
    sz0B = [region(f"sz0{j}", 1024 * 4, at=uTB[3].start + j * 4096) for j in range(2)]
    pos[0] = phase0
    wslB = region("wslab", 8 * 768 * 2)
    qTB = region("qT", 2048 * 2)
    kTB = [region(f"kT{j}", 2048 * 2) for j in range(2)]
    vSB = [region(f"vS{j}", 16 * 128 * 2) for j in range(2)]
    szTB = [region(f"szT{j}", 1024 * 4) for j in range(2)]
    szTB_alt = [region("szTa0", 1024 * 4, at=gbB.start), region("szTa1", 1024 * 4, at=uB[0].start)]
    DEPTH = int(os.environ.get("KDEPTH", "5"))
    NOM, NR, NA = DEPTH + 1, 3, 3
    omB = [region(f"om{j}", 513 * 4) for j in range(NOM)]
    RB = [region(f"R{j}", 513 * 4) for j in range(NR)]
    AB = [region(f"A{j}", 512 * 2) for j in range(NA)]
    ATB = [region(f"AT{j}", 16 * 256 * 2) for j in range(2)]
    end_l1 = pos[0]
    total = max(end_l0, end_l1)
    for i, a in enumerate(bufs):
        for b in bufs[i + 1:]:
            if a.start < b.start + b.size and b.start < a.start + a.size:
                a.al.append(b)
                b.al.append(a)

    arena = nc.alloc_sbuf_tensor("arena", [128, total], U8)

    def view(b, dt, pat=None, parts=128, **kw):
        v = arena[0:parts, b.start:b.start + b.size].bitcast(dt)
        if pat:
            v = v.rearrange(pat, **kw)
        return v

    hV = [view(b, F32) for b in hB]
    h4V = [arena[:, hB[4 * j].start:hB[4 * j].start + 4 * D * 4].bitcast(F32)
           .rearrange("p (t d) -> p t d", t=4) for j in range(4)]
    uTV = [view(b, BF16, "p (k n) -> p k n", k=8) for b in uTB]
    gbV = view(gbB, F32)
    yTV = [view(b, BF16) for b in yTB]
    yT0V = [arena[:, yTB[2 * j].start:yTB[2 * j].start + 8192].bitcast(BF16)
            .rearrange("p (k n) -> p k n", k=4) for j in range(2)]
    yT0B = [[yTB[0], yTB[1]], [yTB[2], yTB[3]]]
    woV = [view(b, BF16, "p (k n) -> p k n", k=4) for b in woB]
    uV = [view(b, BF16) for b in uB]
    junkB, junkV = uB[0], uV[0]
    identV = view(identB, BF16)
    zerosV = view(zerosB, F32)
    rcV, pscV, msqV, rstdV = view(rcB, F32), view(pscB, F32), view(msqB, F32), view(rstdB, F32)
    xbTV = view(xbTB, F32, "p (j n) -> p j n", j=4)
    tmpV = [view(b, F32) for b in tmpB_]
    winV = [view(b, BF16, "p (k n) -> p k n", k=8) for b in winB]
    wgV = [view(b, BF16, "p (k n) -> p k n", k=4) for b in wgB]
    haloV = view(haloB, F32, "p (c n) -> p c n", c=16)
    fixV = view(fixB, F32)
    pooledV = view(pooledB, BF16, "p (k n) -> p k n", k=4)
    sz0V = [view(b, F32) for b in sz0B]
    wslV = view(wslB, BF16, "p (k n) -> p k n", k=8)
    qTV = view(qTB, BF16)
    kTV = [view(b, BF16) for b in kTB]
    vSV = [view(b, BF16, "p (t n) -> p t n", t=16) for b in vSB]
    szTV = [view(b, F32) for b in szTB]
    szTB2 = [szTB, szTB_alt]
    szTV2 = [szTV, [view(b, F32) for b in szTB_alt]]
    omV = [view(b, F32) for b in omB]
    RV = [view(b, F32) for b in RB]
    AV_ = [view(b, BF16) for b in AB]
    ATV = [view(b, BF16, "p (c n) -> p c n", c=16) for b in ATB]

    psum = nc.alloc_psum_tensor("psum", [128, 8 * 512], F32)
    psum_bf = psum[:, :].bitcast(BF16)
    bankB = [Buf(f"bank{i}") for i in range(8)]
    bptr = [0]

    NBANK = 6

    def alloc_banks(n):
        while n == 2 and (bptr[0] % 2 or bptr[0] + 1 >= NBANK):
            bptr[0] = (bptr[0] + 1) % NBANK
        b0 = bptr[0]
        bptr[0] = (b0 + n) % NBANK
        return b0

    pzB = Buf("pzero")
    oB = [Buf("o_half0"), Buf("o_half1")]
    oV = [psum[:, 6 * 512 + hf * 256:6 * 512 + (hf + 1) * 256] for hf in range(2)]
    pzV = psum[:, 7 * 512:7 * 512 + 1]

    regcache = {}

    def preg(e, val):
        if val not in regcache:
            regcache[val] = e.to_reg(val)
        return regcache[val]

    def mm(out, lhsT, rhs, start, stop, reads, writes, sig):
        S.op("pe", lambda e: e.matmul(out, lhsT, rhs, start=start, stop=stop), reads, writes, sig)

    def act(out, in_, func, reads, writes, **kw):
        S.op("act", lambda e: e.activation(out, in_, func, **kw), reads, writes)

    def dma_w(out, in_, semkey, writes):
        S.dma("pool", lambda e: e.dma_start(out=out, in_=in_), semkey, (), writes)

    def dma_sp(out, in_, semkey, reads=(), writes=()):
        S.dma("sp", lambda e: e.dma_start(out=out, in_=in_), semkey, reads, writes)

    S.op("pool", lambda e: e.memset(junkV[:, 0:128], 1.0), (), [junkB])
    S.op("pool", lambda e: e.affine_select(identV, junkV[:, 0:128], [[-1, 128]], ALU.is_equal, preg(e, 0.0),
                                           base=0, channel_multiplier=1), [junkB], [identB])
    S.op("pool", lambda e: e.memset(zerosV, 0.0), (), [zerosB])
    S.op("dve", lambda e: e.tensor_copy(psum[:, 7 * 512:7 * 512 + 8], zerosV[:, 0:8]), [zerosB], [pzB])
    for t in range(16):
        S.op("pool", (lambda t: lambda e: e.memset(rcV[:, t:t + 1], 1.0 / (t + 1)))(t), (), [rcB])
    dma_sp(pscV, psc_d[:, :], "c_psc", (), [pscB])

    def rms_stats(tiles):
        t0, t1 = tiles[0], tiles[-1] + 1
        for t in tiles:
            act(junkV, hV[t], AF.Square, [hB[t]], [junkB, msqB], scale=1.0 / 32.0, accum_out=msqV[:, t:t + 1])
        S.op("dve", lambda e: e.tensor_scalar(rstdV[:, t0:t1], msqV[:, t0:t1], EPS, None, ALU.add), [msqB], [rstdB])
        S.op("act", lambda e: e.sqrt(rstdV[:, t0:t1], rstdV[:, t0:t1]), [rstdB], [rstdB])
        S.op("dve", lambda e: e.reciprocal(rstdV[:, t0:t1], rstdV[:, t0:t1]), [rstdB], [rstdB])

    def rmsnorm_tile(t, out_ap, out_bufs):
        S.op("dve", lambda e: e.scalar_tensor_tensor(out_ap, hV[t], rstdV[:, t:t + 1], gbV, ALU.mult, ALU.mult),
             [hB[t], rstdB, gbB], out_bufs)

    def make_uT(t, blk, coff):
        us = t % 2
        rmsnorm_tile(t, uV[us], [uB[us]])
        b = alloc_banks(1)
        for k in range(8):
            S.op("pe", (lambda k: lambda e: e.transpose(psum_bf[:, b * 1024 + k * 128: b * 1024 + (k + 1) * 128],
                                                        uV[us][:, k * 128:(k + 1) * 128], identV))(k),
                 [uB[us], identB], [bankB[b]], sig=(k == 7))
        act(uTV[blk][:, :, coff:coff + 128],
            psum_bf[:, b * 1024:(b + 1) * 1024].rearrange("p (k n) -> p k n", k=8),
            AF.Copy, [bankB[b]], [uTB[blk]])

    def load_x(s, j):
        src = x_d[s, 512 * j:512 * (j + 1), :].rearrange("(t p) d -> p t d", p=128)
        dma_sp(h4V[j], src, f"hx{j}", (), hB[4 * j:4 * j + 4])

    def store_h(s, t):
        dma_sp(out_d[s, 128 * t:128 * (t + 1), :], hV[t], f"hs{t // 4}", [hB[t]], ())

    def out_proj(yk_list, yk_bufs, wo_slot, tiles, tok0):
        nk = len(yk_list)
        for t in tiles:
            for n in range(2):
                b = alloc_banks(1)
                po = psum[:, b * 512:(b + 1) * 512]
                for k in range(nk):
                    c0 = (t - tok0) * 128
                    mm(po, yk_list[k][:, c0:c0 + 128], woV[wo_slot][:, k, n * 512:(n + 1) * 512],
                       k == 0, k == nk - 1, yk_bufs + [woB[wo_slot]], [bankB[b]], k == nk - 1)
                hv = hV[t][:, n * 512:(n + 1) * 512]
                S.op("dve", (lambda hv, po: lambda e: e.tensor_tensor(hv, hv, po, ALU.add))(hv, po),
                     [bankB[b], hB[t]], [hB[t]])

    for s in range(2):
        for j in range(4):
            load_x(s, j)

        if do_l0:
            dma_sp(gbV, g0b_d[:, :], "c_gb", (), [gbB])
            win_i = [0]
            pending = None
            for hs in range(2):
                tiles = list(range(hs * 8, hs * 8 + 8))
                rms_stats(tiles)
                for t in tiles:
                    make_uT(t, (t - hs * 8) // 4, ((t - hs * 8) % 4) * 128)
                for g in range(4):
                    w = POOL_W[g]
                    sx, sz = win_i[0] % 3, (win_i[0] + 1) % 3
                    win_i[0] += 2
                    gs = g % 2
                    dma_w(winV[sx], w0in_d[g, :, :].rearrange("p (k n) -> p k n", k=8), f"win{sx}", [winB[sx]])
                    dma_w(winV[sz], w0in_d[4 + g, :, :].rearrange("p (k n) -> p k n", k=8), f"win{sz}", [winB[sz]])
                    dma_w(wgV[gs], w0g_d[g, :, :].rearrange("p (k n) -> p k n", k=4), f"wg{gs}", [wgB[gs]])
                    if hs == 0:
                        S.op("dve", lambda e: e.memset(xbTV[:, :, 0:16], 0.0), (), [xbTB])
                    else:
                        S.op("dve", (lambda g: lambda e: e.tensor_copy(xbTV[:, :, 0:16], haloV[:, 4 * g:4 * g + 4, :]))(g),
                             [haloB], [xbTB])
                    for j in range(4):
                        b = alloc_banks(2)
                        for n in range(2):
                            for k in range(8):
                                mm(psum[:, (b + n) * 512:(b + n + 1) * 512], winV[sx][:, k, j * 128:(j + 1) * 128],
                                   uTV[n][:, k, :], k == 0, k == 7, [winB[sx], uTB[n]], [bankB[b + n]], k == 7)
                        act(xbTV[:, j, 16:1040], psum[:, b * 512:(b + 2) * 512], AF.Copy,
                            [bankB[b], bankB[b + 1]], [xbTB])
                    if pending is not None:
                        pending()
                        pending = None
                    dma_w(woV[0], w0o_d[g, :, :].rearrange("p (k n) -> p k n", k=4), "wo0", [woB[0]])
                    for j in range(4):
                        X = xbTV[:, j, :]
                        src, lo = X, 0
                        step, ti = 1, 0
                        while step < w:
                            dst = tmpV[ti]
                            nlo = lo + step
                            S.op("dve", (lambda dst, src, nlo, step: lambda e: e.tensor_tensor(
                                dst[:, nlo:1040], src[:, nlo:1040], src[:, nlo - step:1040 - step], ALU.add))(dst, src, nlo, step),
                                 [xbTB, tmpB_[1 - ti]] if src is not X else [xbTB], [tmpB_[ti]])
                            src, lo = dst, nlo
                            step *= 2
                            ti = 1 - ti
                        last = 1 - ti
                        S.op("dve", (lambda src, X, j, w: lambda e: e.scalar_tensor_tensor(
                            pooledV[:, j, :], src[:, 16:1040], 1.0 / w, X[:, 16:1040], ALU.mult, ALU.subtract))(src, X, j, w),
                             [tmpB_[last], xbTB], [pooledB])
                        if hs == 0:
                            S.op("dve", (lambda src, w: lambda e: e.tensor_tensor(
                                fixV[:, 0:w - 1], src[:, 16:16 + w - 1], rcV[:, 0:w - 1], ALU.mult))(src, w),
                                 [tmpB_[last], rcB], [fixB])
                            S.op("dve", (lambda X, j, w: lambda e: e.tensor_tensor(
                                pooledV[:, j, 0:w - 1], fixV[:, 0:w - 1], X[:, 16:16 + w - 1], ALU.subtract))(X, j, w),
                                 [fixB, xbTB], [pooledB])
                        if hs == 0:
                            S.op("dve", (lambda X, g, j: lambda e: e.tensor_copy(haloV[:, 4 * g + j, :], X[:, 1024:1040]))(X, g, j),
                                 [xbTB], [haloB])
                    for j in range(4):
                        zs = j % 2
                        b = alloc_banks(2)
                        for n in range(2):
                            for k in range(8):
                                mm(psum[:, (b + n) * 512:(b + n + 1) * 512], winV[sz][:, k, j * 128:(j + 1) * 128],
                                   uTV[n][:, k, :], k == 0, k == 7, [winB[sz], uTB[n]], [bankB[b + n]], k == 7)
                        act(sz0V[zs], psum[:, b * 512:(b + 2) * 512], AF.Silu, [bankB[b], bankB[b + 1]], [sz0B[zs]])
                        b2 = alloc_banks(2)
                        for n in range(2):
                            for k in range(4):
                                mm(psum[:, (b2 + n) * 512:(b2 + n + 1) * 512], wgV[gs][:, k, j * 128:(j + 1) * 128],
                                   pooledV[:, k, n * 512:(n + 1) * 512], k == 0, k == 3, [wgB[gs], pooledB],
                                   [bankB[b2 + n]], k == 3)
                        c = 4 * g + j
                        S.op("dve", (lambda b2, c, j, zs, gs: lambda e: e.scalar_tensor_tensor(
                            yT0V[gs][:, j, :], psum[:, b2 * 512:(b2 + 2) * 512], pscV[:, c:c + 1], sz0V[zs],
                            ALU.mult, ALU.mult))(b2, c, j, zs, gs),
                             [bankB[b2], bankB[b2 + 1], pscB, sz0B[zs]], yT0B[gs])
                    pending = (lambda gs, tiles, hs: lambda: out_proj(
                        [yT0V[gs][:, k, :] for k in range(4)], yT0B[gs], 0, tiles, hs * 8))(gs, tiles, hs)
                pending()
                pending = None

        if mode == "l0":
            for t in range(NT):
                store_h(s, t)
            continue

        dma_sp(gbV, g1b_d[:, :], "c_gb", (), [gbB])
        rms_stats(list(range(NT)))
        for t in range(NT):
            make_uT(t, t // 4, (t % 4) * 128)
        for j in range(NOM):
            S.op("pool", (lambda j: lambda e: e.memset(omV[j][:, 512:513], 1.0))(j), (), [omB[j]])
        S.op("pool", lambda e: e.memset(kTV[0][64:128, :], 0.0), (), [kTB[0]])
        S.op("pool", lambda e: e.memset(kTV[1][0:64, :], 0.0), (), [kTB[1]])

        seg_i = [0]

        def proj_qk(hp):
            for half in range(2):
                for which in ("q", "k"):
                    col0 = 0 if which == "q" else 128
                    b = alloc_banks(2)
                    for n in range(2):
                        blk = half * 2 + n
                        for k in range(8):
                            mm(psum[:, (b + n) * 512:(b + n + 1) * 512], wslV[:, k, col0:col0 + 128],
                               uTV[blk][:, k, :], k == 0, k == 7, [wslB, uTB[blk]], [bankB[b + n]], k == 7)
                    cs = slice(half * 1024, (half + 1) * 1024)
                    if which == "q":
                        act(qTV[:, cs], psum[:, b * 512:(b + 2) * 512], AF.Copy,
                            [bankB[b], bankB[b + 1]], [qTB], scale=0.125)
                    else:
                        act(kTV[0][0:64, cs], psum[0:64, b * 512:(b + 2) * 512], AF.Copy,
                            [bankB[b], bankB[b + 1]], [kTB[0]])
                        act(kTV[1][64:128, cs], psum[64:128, b * 512:(b + 2) * 512], AF.Copy,
                            [bankB[b], bankB[b + 1]], [kTB[1]])

        def proj_z(head):
            hh = head % 2
            for half in range(2):
                b = alloc_banks(2)
                for n in range(2):
                    blk = half * 2 + n
                    for k in range(8):
                        mm(psum[:, (b + n) * 512:(b + n + 1) * 512], wslV[:, k, 512 + hh * 128:512 + (hh + 1) * 128],
                           uTV[blk][:, k, :], k == 0, k == 7, [wslB, uTB[blk]], [bankB[b + n]], k == 7)
                act(szTV2[hh][half], psum[:, b * 512:(b + 2) * 512], AF.Silu, [bankB[b], bankB[b + 1]], [szTB2[hh][half]])

        def v_half_unit(head, tq, part):
            hh, vs = head % 2, head % 2
            b = alloc_banks(1)
            for j, tt in enumerate((2 * part, 2 * part + 1)):
                for k in range(8):
                    mm(psum[:, b * 512 + j * 128:b * 512 + (j + 1) * 128],
                       uTV[tq][:, k, tt * 128:(tt + 1) * 128], wslV[:, k, 256 + hh * 128:256 + (hh + 1) * 128],
                       k == 0, k == 7, [wslB, uTB[tq]], [bankB[b]], (k == 7 and j == 1))
            t0 = tq * 4 + 2 * part
            act(vSV[vs][:, t0:t0 + 2, :],
                psum[:, b * 512:b * 512 + 256].rearrange("p (t n) -> p t n", t=2), AF.Copy,
                [bankB[b]], [vSB[vs]])

        for head in range(16):
            if True:
                hp, hh = divmod(head, 2)
                grp, hin = divmod(head, 4)
                vSc = head % 2
                if head == 0:
                    dma_w(wslV, w1in_d[0, :, :].rearrange("p (k n) -> p k n", k=8), "wsl", [wslB])
                    proj_qk(0)
                    proj_z(0)
                    for tq in range(4):
                        v_half_unit(0, tq, 0)
                        v_half_unit(0, tq, 1)
                if hin == 0:
                    dma_w(woV[0], w1o_d[grp, :, :].rearrange("p (k n) -> p k n", k=4), "wo0", [woB[0]])
                if hh == 1 and hp < 7:
                    dma_w(wslV, w1in_d[hp + 1, :, :].rearrange("p (k n) -> p k n", k=8), "wsl", [wslB])
                filler = [(head + 1, tq, part) for tq in range(4) for part in range(2)] if head < 15 else []

                segs = []
                for sb2 in range(8):
                    for i in (2 * sb2, 2 * sb2 + 1):
                        nseg = i // 4 + 1
                        for sg in reversed(range(nseg)):
                            diag = (sg == nseg - 1)
                            w = ((i % 4) + 1) * 128 if diag else 512
                            segs.append(dict(sb2=sb2, i=i, sg=sg, w=w, diag=diag, hin=hin, hh=hh,
                                             last=(sg == 0 and i == 2 * sb2 + 1)))
                prev = None
                for sd in segs:
                    sd["n"] = seg_i[0]
                    seg_i[0] += 1
                    sd["prev"] = None if sd["diag"] else prev
                    prev = sd

                def stage_a(sd):
                    i, sg, w, sl = sd["i"], sd["sg"], sd["w"], sd["n"] % NOM
                    b = alloc_banks(1)
                    zp = psum[:, b * 512:b * 512 + w]
                    if os.environ.get("KZPAD", "1") == "1":
                        mm(zp, qTV[:, i * 128:(i + 1) * 128], kTV[sd["hh"]][:, sg * 512:sg * 512 + w], True, True,
                           [qTB, kTB[sd["hh"]]], [bankB[b]], True)
                    else:
                        p0 = sd["hh"] * 64
                        mm(zp, qTV[p0:p0 + 64, i * 128:(i + 1) * 128], kTV[sd["hh"]][p0:p0 + 64, sg * 512:sg * 512 + w],
                           True, True, [qTB, kTB[sd["hh"]]], [bankB[b]], True)
                    act(omV[sl][:, 512 - w:512], zp, AF.Sigmoid, [bankB[b]], [omB[sl]], scale=-1.0)

                def stage_m(sd):
                    if sd["diag"]:
                        sl = sd["n"] % NOM
                        S.op("pool", lambda e: e.affine_select(omV[sl][:, 384:512], omV[sl][:, 384:512], [[-1, 128]],
                                                               ALU.is_gt, preg(e, 1.0), base=0, channel_multiplier=1),
                             [omB[sl]], [omB[sl]])

                def stage_b(sd):
                    i, sg, w = sd["i"], sd["sg"], sd["w"]
                    so, sr, sa = sd["n"] % NOM, sd["n"] % NR, sd["n"] % NA
                    lo = 512 - w
                    if sd["prev"] is None:
                        init, rds = 1.0, [omB[so], pzB]
                    else:
                        p = sd["prev"]
                        plo, pr = 512 - p["w"], p["n"] % NR
                        init, rds = RV[pr][:, plo:plo + 1], [omB[so], pzB, RB[pr]]
                    S.op("dve", lambda e: e.tensor_tensor_scan(RV[sr][:, lo:513][:, ::-1], omV[so][:, lo:513][:, ::-1],
                                                               pzV.broadcast_to([128, w + 1]), init, ALU.mult, ALU.add),
                         rds, [RB[sr]])
                    S.op("pool", lambda e: e.tensor_tensor(AV_[sa][:, 0:w], RV[sr][:, lo + 1:513], RV[sr][:, lo:512],
                                                           ALU.subtract), [RB[sr]], [AB[sa]])
                    sl = sa
                    nb = w // 128
                    b = alloc_banks(1)
                    for blk in range(nb):
                        S.op("pe", (lambda blk: lambda e: e.transpose(
                            psum_bf[:, b * 1024 + blk * 128:b * 1024 + (blk + 1) * 128],
                            AV_[sl][:, blk * 128:(blk + 1) * 128], identV))(blk),
                             [AB[sl], identB], [bankB[b]], sig=(blk == nb - 1))
                    ats = sd["sb2"] % 2
                    qo = (i % 2) * 128
                    act(ATV[ats][:, 4 * sg:4 * sg + nb, qo:qo + 128],
                        psum_bf[:, b * 1024:b * 1024 + w].rearrange("p (c n) -> p c n", c=nb), AF.Copy,
                        [bankB[b]], [ATB[ats]])

                def stage_c(sd):
                    sb2 = sd["sb2"]
                    ats, hf = sb2 % 2, sb2 % 2
                    nch = 2 * sb2 + 2
                    for c in range(nch):
                        lo2 = 128 if c == nch - 1 else 0
                        mm(oV[hf][:, lo2:256], vSV[vSc][:, c, :], ATV[ats][:, c, lo2:256],
                           c == 0, c == nch - 1, [vSB[vSc], ATB[ats]], [oB[hf]], c == nch - 1)

                def stage_d(sd):
                    sb2 = sd["sb2"]
                    hf = sb2 % 2
                    half, co = sb2 // 4, (sb2 % 4) * 256
                    y_out = yTV[sd["hin"]][:, sb2 * 256:(sb2 + 1) * 256]
                    zs = sd["hh"]
                    S.op("dve", lambda e: e.tensor_tensor(y_out, oV[hf], szTV2[zs][half][:, co:co + 256], ALU.mult),
                         [oB[hf], szTB2[zs][half]], [yTB[sd["hin"]]])

                last_of = {sd["sb2"]: k for k, sd in enumerate(segs) if sd["last"]}
                c_at = {}
                for sb2, k in last_of.items():
                    c_at.setdefault(min(k + (2 if sb2 == 0 else 3), len(segs) - 1), []).append(sb2)
                done_c, done_d = set(), set()

                def emit_c(sb2):
                    if sb2 >= 2 and (sb2 - 2) not in done_d:
                        stage_d(segs[last_of[sb2 - 2]])
                        done_d.add(sb2 - 2)
                    stage_c(segs[last_of[sb2]])
                    done_c.add(sb2)

                for idx in range(len(segs) + DEPTH):
                    if idx < len(segs):
                        stage_a(segs[idx])
                    if idx >= DEPTH:
                        k = idx - DEPTH
                        stage_b(segs[k])
                        for sb2 in c_at.get(k, []):
                            emit_c(sb2)
                    if idx < len(segs):
                        stage_m(segs[idx])
                    if idx == 40:
                        if hh == 1 and hp < 7:
                            proj_qk(hp + 1)
                        if head < 15:
                            proj_z(head + 1)
                    if filler and idx >= (4 if hh == 0 else 16) and idx % 4 == 0:
                        v_half_unit(*filler.pop(0))
                while filler:
                    v_half_unit(*filler.pop(0))
                for sb2 in range(8):
                    if sb2 not in done_c:
                        emit_c(sb2)
                for sb2 in range(8):
                    if sb2 not in done_d:
                        stage_d(segs[last_of[sb2]])
                        done_d.add(sb2)

                if hin == 3:
                    out_proj([yTV[k] for k in range(4)], list(yTB), 0, list(range(NT)), 0)

        dma_sp(gbV, gfb_d[:, :], "c_gb", (), [gbB])
        rms_stats(list(range(NT)))
        for t in range(NT):
            rmsnorm_tile(t, hV[t], [hB[t]])
            store_h(s, t)

    sems = {}
    for k in list(COMPUTE) + list(S.dcnt.keys()):
        sems[k] = nc.alloc_semaphore(name=f"s_{k}")
    fin = {k: v for k, v in S.dcnt.items() if k.startswith("hs")}
    S.prog["sp"].append((fin, None, None))

    def replay(eng_name, e):
        for waits, emit, sig in S.prog[eng_name]:
            for k, v in waits.items():
                e.wait_ge(sems[k], v)
            if emit is None:
                continue
            ins = emit(e)
            if sig is not None:
                ins.then_inc(sems[sig], 1 if sig in COMPUTE else 16)

    with nc.Block() as block:
        @block.sync
        def _(e):
            replay("sp", e)

        @block.gpsimd
        def _(e):
            replay("pool", e)

        @block.scalar
        def _(e):
            replay("act", e)

        @block.vector
        def _(e):
            replay("dve", e)

        @block.tensor
        def _(e):
            replay("pe", e)
    return nc


_PROGS = {}


def _prog(mode):
    if mode not in _PROGS:
        _PROGS[mode] = build_program(mode)
    return _PROGS[mode]


def _layout_weights(norm_g, pool_w_in, pool_w, pool_scale, pool_w_out, sb_w_in, sb_w_out, norm_f):
    f = np.float32
    c = np.ascontiguousarray
    w0 = np.asarray(pool_w_in, f)[0]
    w0in = c(w0.reshape(8, 128, 8, 512).transpose(2, 1, 0, 3)).reshape(8, 128, 8 * 512)
    wg = np.asarray(pool_w, f)[0]
    w0g = c(wg.reshape(4, 4, 128, 512).transpose(0, 2, 1, 3)).reshape(4, 128, 4 * 512)
    wo0 = np.asarray(pool_w_out, f)[0]
    w0o = c(wo0.reshape(4, 4, 128, 1024).transpose(0, 2, 1, 3)).reshape(4, 128, 4 * 1024)
    w1 = np.asarray(sb_w_in, f)[0]
    q = w1[:, 0:1024].reshape(8, 128, 8, 128)
    kk = w1[:, 1024:2048].reshape(8, 128, 8, 128)
    v = w1[:, 2048:4096].reshape(8, 128, 8, 256)
    z = w1[:, 4096:6144].reshape(8, 128, 8, 256)
    slab = np.concatenate([q, kk, v, z], axis=3)
    w1in = c(slab.transpose(2, 1, 0, 3)).reshape(8, 128, 8 * 768)
    wo1 = np.asarray(sb_w_out, f)[0]
    w1o = c(wo1.reshape(4, 4, 128, 1024).transpose(0, 2, 1, 3)).reshape(4, 128, 4 * 1024)
    ng = np.asarray(norm_g, f)
    return {
        "g0b": c(np.broadcast_to(ng[0], (128, D))), "g1b": c(np.broadcast_to(ng[1], (128, D))),
        "gfb": c(np.broadcast_to(np.asarray(norm_f, f), (128, D))),
        "pscale": c(np.asarray(pool_scale, f)[0].reshape(16, 128).T),
        "w0in": w0in, "w0g": w0g, "w0o": w0o, "w1in": w1in, "w1o": w1o,
    }


FUSED = True


def kernel(x, norm_g, pool_w_in, pool_w, pool_scale, pool_w_out, sb_w_in, sb_w_out, norm_f):
    x = np.ascontiguousarray(np.asarray(x, np.float32))
    wmap = _layout_weights(norm_g, pool_w_in, pool_w, pool_scale, pool_w_out, sb_w_in, sb_w_out, norm_f)
    cores = list(range(NCORES))
    if FUSED:
        in_maps = [dict(wmap, x=x[2 * c:2 * c + 2]) for c in cores]
        res = run_bass_kernel_spmd(_prog("full"), in_maps, core_ids=cores)
        return np.concatenate([r["out"] for r in res.results], axis=0)
    in_maps = [dict(wmap, x=x[2 * c:2 * c + 2]) for c in cores]
    res = run_bass_kernel_spmd(_prog("l0"), in_maps, core_ids=cores)
    in_maps = [dict(wmap, x=np.ascontiguousarray(r["out"])) for r in res.results]
    res = run_bass_kernel_spmd(_prog("l1"), in_maps, core_ids=cores)
    return np.concatenate([r["out"] for r in res.results], axis=0)
```

```python
import os
import numpy as np
import concourse.bass as bass
import concourse.mybir as mybir
from concourse.bass_utils import run_bass_kernel_spmd

F32 = mybir.dt.float32
BF16 = mybir.dt.bfloat16
U8 = mybir.dt.uint8
ALU = mybir.AluOpType
AF = mybir.ActivationFunctionType

NCORES = 8
SEQ = 2048
D = 1024
NT = SEQ // 128
EPS = 1e-6
POOL_W = (2, 4, 8, 16)
COMPUTE = ("pe", "act", "dve", "pool")


class Buf:
    __slots__ = ("name", "start", "size", "lw", "rd", "al")

    def __init__(self, name, start=None, size=0):
        self.name, self.start, self.size = name, start, size
        self.lw = {}
        self.rd = {}
        self.al = [self]


class Sched:
    def __init__(self):
        self.prog = {e: [] for e in COMPUTE + ("sp",)}
        self.cnt = {e: 0 for e in COMPUTE}
        self.seen = {e: {} for e in COMPUTE + ("sp",)}
        self.dcnt = {}

    def _waits(self, eng, reads, writes):
        waits = {}
        seen = self.seen[eng]

        def need(k, v, raw):
            if k == eng and (eng == "pe" or not raw):
                return
            if seen.get(k, 0) >= v:
                return
            if waits.get(k, 0) < v:
                waits[k] = v

        for b in reads:
            for a in b.al:
                for k, v in a.lw.items():
                    need(k, v, True)
        for b in writes:
            for a in b.al:
                for k, v in a.lw.items():
                    need(k, v, False)
                for k, v in a.rd.items():
                    need(k, v, False)
        for k, v in waits.items():
            seen[k] = v
        return waits

    def op(self, eng, emit, reads=(), writes=(), sig=True):
        waits = self._waits(eng, reads, writes)
        if sig:
            self.cnt[eng] += 1
            val = self.cnt[eng]
        else:
            val = self.cnt[eng] + 1
        self.prog[eng].append((waits, emit, eng if sig else None))
        for b in reads:
            b.rd[eng] = val
        for b in writes:
            b.lw[eng] = val

    def dma(self, q, emit, semkey, reads=(), writes=()):
        waits = self._waits(q, reads, writes)
        self.dcnt[semkey] = self.dcnt.get(semkey, 0) + 16
        val = self.dcnt[semkey]
        self.prog[q].append((waits, emit, semkey))
        for b in reads:
            b.rd[semkey] = val
        for b in writes:
            b.lw[semkey] = val


def build_program(mode="full"):
    do_l0 = mode in ("full", "l0")
    do_l1 = mode in ("full", "l1")
    nc = bass.Bass("TRN2", target_bir_lowering=False)
    S = Sched()

    def din(name, shape):
        return nc.dram_tensor(name, shape, F32, kind="ExternalInput").ap()

    x_d = din("x", [2, SEQ, D])
    out_d = nc.dram_tensor("out", [2, SEQ, D], F32, kind="ExternalOutput").ap()
    g0b_d, g1b_d, gfb_d = din("g0b", [128, D]), din("g1b", [128, D]), din("gfb", [128, D])
    psc_d = din("pscale", [128, 16])
    w0in_d = din("w0in", [8, 128, 8 * 512])
    w0g_d = din("w0g", [4, 128, 4 * 512])
    w0o_d = din("w0o", [4, 128, 4 * 1024])
    w1in_d = din("w1in", [8, 128, 8 * 768])
    w1o_d = din("w1o", [4, 128, 4 * 1024])

    bufs = []
    pos = [0]

    def region(name, size, at=None):
        if at is None:
            at = pos[0]
            pos[0] = at + ((size + 31) // 32) * 32
        b = Buf(name, at, size)
        bufs.append(b)
        return b

    hB = [region(f"h{t}", D * 4) for t in range(NT)]
    uTB = [region(f"uT{j}", 8 * 512 * 2) for j in range(4)]
    gbB = region("gb", D * 4)
    uB = [region(f"u{j}", D * 2) for j in range(2)]
    yTB = [region(f"yT{j}", 2048 * 2) for j in range(4)]
    woB = [region("wo0", 4 * 1024 * 2)]
    identB = region("ident", 128 * 2)
    zerosB = region("zeros", 8 * 4)
    rcB = region("rc16", 16 * 4)
    pscB = region("psc", 16 * 4)
    msqB = region("msq", 16 * 4)
    rstdB = region("rstd", 16 * 4)
    phase0 = pos[0]
    xbTB = region("xbT", 4 * 1040 * 4)
    tmpB_ = [region(f"tmp{j}", 1040 * 4) for j in range(2)]
    winB = [region(f"win{j}", 8 * 512 * 2) for j in range(3)]
    wgB = [region(f"wg{j}", 4 * 512 * 2) for j in range(2)]
    haloB = region("halo", 16 * 16 * 4)
    fixB = region("fix", 16 * 4)
    sz0x = [region(f"sz0{j}", 1024 * 4) for j in (2, 3)]
    end_l0 = pos[0]
    pooledB = region("pooled", 4 * 1024 * 2, at=uTB[2].start)
    sz0B = [region(f"sz0{j}", 1024 * 4, at=uTB[3].start + j * 4096) for j in range(2)] + sz0x
    pos[0] = phase0
    wslB = region("wslab", 8 * 768 * 2)
    qTB = region("qT", 2048 * 2)
    kTB = [region(f"kT{j}", 2048 * 2) for j in range(2)]
    vSB = [region(f"vS{j}", 16 * 128 * 2) for j in range(2)]
    szTB = [region(f"szT{j}", 1024 * 4) for j in range(2)]
    szTB_alt = [region("szTa0", 1024 * 4, at=gbB.start), region("szTa1", 1024 * 4, at=uB[0].start)]
    DEPTH = int(os.environ.get("KDEPTH", "5"))
    NOM, NR, NA = DEPTH + 1, 3, 3
    omB = [region(f"om{j}", 513 * 4) for j in range(NOM)]
    RB = [region(f"R{j}", 513 * 4) for j in range(NR)]
    AB = [region(f"A{j}", 512 * 2) for j in range(NA)]
    ATB = [region(f"AT{j}", 16 * 256 * 2) for j in range(2)]
    end_l1 = pos[0]
    total = max(end_l0, end_l1)
    for i, a in enumerate(bufs):
        for b in bufs[i + 1:]:
            if a.start < b.start + b.size and b.start < a.start + a.size:
                a.al.append(b)
                b.al.append(a)

    arena = nc.alloc_sbuf_tensor("arena", [128, total], U8)

    def view(b, dt, pat=None, parts=128, **kw):
        v = arena[0:parts, b.start:b.start + b.size].bitcast(dt)
        if pat:
            v = v.rearrange(pat, **kw)
        return v

    hV = [view(b, F32) for b in hB]
    h4V = [arena[:, hB[4 * j].start:hB[4 * j].start + 4 * D * 4].bitcast(F32)
           .rearrange("p (t d) -> p t d", t=4) for j in range(4)]
    uTV = [view(b, BF16, "p (k n) -> p k n", k=8) for b in uTB]
    gbV = view(gbB, F32)
    yTV = [view(b, BF16) for b in yTB]
    yT0V = [arena[:, yTB[2 * j].start:yTB[2 * j].start + 8192].bitcast(BF16)
            .rearrange("p (k n) -> p k n", k=4) for j in range(2)]
    yT0B = [[yTB[0], yTB[1]], [yTB[2], yTB[3]]]
    woV = [view(b, BF16, "p (k n) -> p k n", k=4) for b in woB]
    uV = [view(b, BF16) for b in uB]
    junkB, junkV = uB[0], uV[0]
    identV = view(identB, BF16)
    zerosV = view(zerosB, F32)
    rcV, pscV, msqV, rstdV = view(rcB, F32), view(pscB, F32), view(msqB, F32), view(rstdB, F32)
    xbTV = view(xbTB, F32, "p (j n) -> p j n", j=4)
    tmpV = [view(b, F32) for b in tmpB_]
    winV = [view(b, BF16, "p (k n) -> p k n", k=8) for b in winB]
    wgV = [view(b, BF16, "p (k n) -> p k n", k=4) for b in wgB]
    haloV = view(haloB, F32, "p (c n) -> p c n", c=16)
    fixV = view(fixB, F32)
    pooledV = view(pooledB, BF16, "p (k n) -> p k n", k=4)
    sz0V = [view(b, F32) for b in sz0B]
    wslV = view(wslB, BF16, "p (k n) -> p k n", k=8)
    qTV = view(qTB, BF16)
    kTV = [view(b, BF16) for b in kTB]
    vSV = [view(b, BF16, "p (t n) -> p t n", t=16) for b in vSB]
    szTV = [view(b, F32) for b in szTB]
    szTB2 = [szTB, szTB_alt]
    szTV2 = [szTV, [view(b, F32) for b in szTB_alt]]
    omV = [view(b, F32) for b in omB]
    RV = [view(b, F32) for b in RB]
    AV_ = [view(b, BF16) for b in AB]
    ATV = [view(b, BF16, "p (c n) -> p c n", c=16) for b in ATB]

    psum = nc.alloc_psum_tensor("psum", [128, 8 * 512], F32)
    psum_bf = psum[:, :].bitcast(BF16)
    bankB = [Buf(f"bank{i}") for i in range(8)]
    bptr = [0]

    NBANK = 6

    def alloc_banks(n):
        while n == 2 and (bptr[0] % 2 or bptr[0] + 1 >= NBANK):
            bptr[0] = (bptr[0] + 1) % NBANK
        b0 = bptr[0]
        bptr[0] = (b0 + n) % NBANK
        return b0

    pzB = Buf("pzero")
    oB = [Buf("o_half0"), Buf("o_half1")]
    oV = [psum[:, 6 * 512 + hf * 256:6 * 512 + (hf + 1) * 256] for hf in range(2)]
    pzV = psum[:, 7 * 512:7 * 512 + 1]

    regcache = {}

    def preg(e, val):
        if val not in regcache:
            regcache[val] = e.to_reg(val)
        return regcache[val]

    def mm(out, lhsT, rhs, start, stop, reads, writes, sig):
        S.op("pe", lambda e: e.matmul(out, lhsT, rhs, start=start, stop=stop), reads, writes, sig)

    def act(out, in_, func, reads, writes, **kw):
        S.op("act", lambda e: e.activation(out, in_, func, **kw), reads, writes)

    def dma_w(out, in_, semkey, writes):
        S.dma("pool", lambda e: e.dma_start(out=out, in_=in_), semkey, (), writes)

    def dma_sp(out, in_, semkey, reads=(), writes=()):
        S.dma("sp", lambda e: e.dma_start(out=out, in_=in_), semkey, reads, writes)

    S.op("pool", lambda e: e.memset(junkV[:, 0:128], 1.0), (), [junkB])
    S.op("pool", lambda e: e.affine_select(identV, junkV[:, 0:128], [[-1, 128]], ALU.is_equal, preg(e, 0.0),
                                           base=0, channel_multiplier=1), [junkB], [identB])
    S.op("pool", lambda e: e.memset(zerosV, 0.0), (), [zerosB])
    S.op("dve", lambda e: e.tensor_copy(psum[:, 7 * 512:7 * 512 + 8], zerosV[:, 0:8]), [zerosB], [pzB])
    for t in range(16):
        S.op("pool", (lambda t: lambda e: e.memset(rcV[:, t:t + 1], 1.0 / (t + 1)))(t), (), [rcB])
    dma_sp(pscV, psc_d[:, :], "c_psc", (), [pscB])

    def rms_stats(tiles):
        t0, t1 = tiles[0], tiles[-1] + 1
        for t in tiles:
            act(junkV, hV[t], AF.Square, [hB[t]], [junkB, msqB], scale=1.0 / 32.0, accum_out=msqV[:, t:t + 1])
        S.op("dve", lambda e: e.tensor_scalar(rstdV[:, t0:t1], msqV[:, t0:t1], EPS, None, ALU.add), [msqB], [rstdB])
        S.op("act", lambda e: e.sqrt(rstdV[:, t0:t1], rstdV[:, t0:t1]), [rstdB], [rstdB])
        S.op("dve", lambda e: e.reciprocal(rstdV[:, t0:t1], rstdV[:, t0:t1]), [rstdB], [rstdB])

    def rmsnorm_tile(t, out_ap, out_bufs):
        S.op("dve", lambda e: e.scalar_tensor_tensor(out_ap, hV[t], rstdV[:, t:t + 1], gbV, ALU.mult, ALU.mult),
             [hB[t], rstdB, gbB], out_bufs)

    def make_uT(t, blk, coff):
        us = t % 2
        rmsnorm_tile(t, uV[us], [uB[us]])
        b = alloc_banks(1)
        for k in range(8):
            S.op("pe", (lambda k: lambda e: e.transpose(psum_bf[:, b * 1024 + k * 128: b * 1024 + (k + 1) * 128],
                                                        uV[us][:, k * 128:(k + 1) * 128], identV))(k),
                 [uB[us], identB], [bankB[b]], sig=(k == 7))
        act(uTV[blk][:, :, coff:coff + 128],
            psum_bf[:, b * 1024:(b + 1) * 1024].rearrange("p (k n) -> p k n", k=8),
            AF.Copy, [bankB[b]], [uTB[blk]])

    def load_x(s, j):
        src = x_d[s, 512 * j:512 * (j + 1), :].rearrange("(t p) d -> p t d", p=128)
        dma_sp(h4V[j], src, f"hx{j}", (), hB[4 * j:4 * j + 4])

    def store_h(s, t):
        dma_sp(out_d[s, 128 * t:128 * (t + 1), :], hV[t], f"hs{t // 4}", [hB[t]], ())

    def out_proj(yk_list, yk_bufs, wo_slot, tiles, tok0):
        nk = len(yk_list)
        for t in tiles:
            for n in range(2):
                b = alloc_banks(1)
                po = psum[:, b * 512:(b + 1) * 512]
                for k in range(nk):
                    c0 = (t - tok0) * 128
                    mm(po, yk_list[k][:, c0:c0 + 128], woV[wo_slot][:, k, n * 512:(n + 1) * 512],
                       k == 0, k == nk - 1, yk_bufs + [woB[wo_slot]], [bankB[b]], k == nk - 1)
                hv = hV[t][:, n * 512:(n + 1) * 512]
                S.op("dve", (lambda hv, po: lambda e: e.tensor_tensor(hv, hv, po, ALU.add))(hv, po),
                     [bankB[b], hB[t]], [hB[t]])

    for s in range(2):
        for j in range(4):
            load_x(s, j)

        if do_l0:
            dma_sp(gbV, g0b_d[:, :], "c_gb", (), [gbB])
            win_i = [0]
            pending = None
            for hs in range(2):
                tiles = list(range(hs * 8, hs * 8 + 8))
                rms_stats(tiles)
                for t in tiles:
                    make_uT(t, (t - hs * 8) // 4, ((t - hs * 8) % 4) * 128)
                for g in range(4):
                    w = POOL_W[g]
                    sx, sz = win_i[0] % 3, (win_i[0] + 1) % 3
                    win_i[0] += 2
                    gs = g % 2
                    dma_w(winV[sx], w0in_d[g, :, :].rearrange("p (k n) -> p k n", k=8), f"win{sx}", [winB[sx]])
                    dma_w(winV[sz], w0in_d[4 + g, :, :].rearrange("p (k n) -> p k n", k=8), f"win{sz}", [winB[sz]])
                    dma_w(wgV[gs], w0g_d[g, :, :].rearrange("p (k n) -> p k n", k=4), f"wg{gs}", [wgB[gs]])
                    if hs == 0:
                        S.op("dve", lambda e: e.memset(xbTV[:, :, 0:16], 0.0), (), [xbTB])
                    else:
                        S.op("dve", (lambda g: lambda e: e.tensor_copy(xbTV[:, :, 0:16], haloV[:, 4 * g:4 * g + 4, :]))(g),
                             [haloB], [xbTB])
                    for j in range(4):
                        b = alloc_banks(2)
                        for n in range(2):
                            for k in range(8):
                                mm(psum[:, (b + n) * 512:(b + n + 1) * 512], winV[sx][:, k, j * 128:(j + 1) * 128],
                                   uTV[n][:, k, :], k == 0, k == 7, [winB[sx], uTB[n]], [bankB[b + n]], k == 7)
                        act(xbTV[:, j, 16:1040], psum[:, b * 512:(b + 2) * 512], AF.Copy,
                            [bankB[b], bankB[b + 1]], [xbTB])
                    for j in range(4):
                        X = xbTV[:, j, :]
                        src, lo = X, 0
                        step, ti = 1, 0
                        while step < w:
                            dst = tmpV[ti]
                            nlo = lo + step
                            S.op("dve", (lambda dst, src, nlo, step: lambda e: e.tensor_tensor(
                                dst[:, nlo:1040], src[:, nlo:1040], src[:, nlo - step:1040 - step], ALU.add))(dst, src, nlo, step),
                                 [xbTB, tmpB_[1 - ti]] if src is not X else [xbTB], [tmpB_[ti]])
                            src, lo = dst, nlo
                            step *= 2
                            ti = 1 - ti
                        last = 1 - ti
                        S.op("dve", (lambda src, X, j, w: lambda e: e.scalar_tensor_tensor(
                            pooledV[:, j, :], src[:, 16:1040], 1.0 / w, X[:, 16:1040], ALU.mult, ALU.subtract))(src, X, j, w),
                             [tmpB_[last], xbTB], [pooledB])
                        if hs == 0:
                            S.op("dve", (lambda src, w: lambda e: e.tensor_tensor(
                                fixV[:, 0:w - 1], src[:, 16:16 + w - 1], rcV[:, 0:w - 1], ALU.mult))(src, w),
                                 [tmpB_[last], rcB], [fixB])
                            S.op("dve", (lambda X, j, w: lambda e: e.tensor_tensor(
                                pooledV[:, j, 0:w - 1], fixV[:, 0:w - 1], X[:, 16:16 + w - 1], ALU.subtract))(X, j, w),
                                 [fixB, xbTB], [pooledB])
                        if hs == 0:
                            S.op("dve", (lambda X, g, j: lambda e: e.tensor_copy(haloV[:, 4 * g + j, :], X[:, 1024:1040]))(X, g, j),
                                 [xbTB], [haloB])
                    for j in range(4):
                        b = alloc_banks(2)
                        for n in range(2):
                            for k in range(8):
                                mm(psum[:, (b + n) * 512:(b + n + 1) * 512], winV[sz][:, k, j * 128:(j + 1) * 128],
                                   uTV[n][:, k, :], k == 0, k == 7, [winB[sz], uTB[n]], [bankB[b + n]], k == 7)
                        act(sz0V[j], psum[:, b * 512:(b + 2) * 512], AF.Silu, [bankB[b], bankB[b + 1]], [sz0B[j]])
                    if pending is not None:
                        pending()
                        pending = None
                    dma_w(woV[0], w0o_d[g, :, :].rearrange("p (k n) -> p k n", k=4), "wo0", [woB[0]])
                    for j in range(4):
                        b2 = alloc_banks(2)
                        for n in range(2):
                            for k in range(4):
                                mm(psum[:, (b2 + n) * 512:(b2 + n + 1) * 512], wgV[gs][:, k, j * 128:(j + 1) * 128],
                                   pooledV[:, k, n * 512:(n + 1) * 512], k == 0, k == 3, [wgB[gs], pooledB],
                                   [bankB[b2 + n]], k == 3)
                        c = 4 * g + j
                        S.op("dve", (lambda b2, c, j, gs: lambda e: e.scalar_tensor_tensor(
                            yT0V[gs][:, j, :], psum[:, b2 * 512:(b2 + 2) * 512], pscV[:, c:c + 1], sz0V[j],
                            ALU.mult, ALU.mult))(b2, c, j, gs),
                             [bankB[b2], bankB[b2 + 1], pscB, sz0B[j]], yT0B[gs])
                    pending = (lambda gs, tiles, hs: lambda: out_proj(
                        [yT0V[gs][:, k, :] for k in range(4)], yT0B[gs], 0, tiles, hs * 8))(gs, tiles, hs)
                pending()
                pending = None

        if mode == "l0":
            for t in range(NT):
                store_h(s, t)
            continue

        dma_sp(gbV, g1b_d[:, :], "c_gb", (), [gbB])
        rms_stats(list(range(NT)))
        for t in range(NT):
            make_uT(t, t // 4, (t % 4) * 128)
        for j in range(NOM):
            S.op("pool", (lambda j: lambda e: e.memset(omV[j][:, 512:513], 1.0))(j), (), [omB[j]])
        S.op("pool", lambda e: e.memset(kTV[0][64:128, :], 0.0), (), [kTB[0]])
        S.op("pool", lambda e: e.memset(kTV[1][0:64, :], 0.0), (), [kTB[1]])

        seg_i = [0]

        def proj_qk(hp):
            for half in range(2):
                for which in ("q", "k"):
                    col0 = 0 if which == "q" else 128
                    b = alloc_banks(2)
                    for n in range(2):
                        blk = half * 2 + n
                        for k in range(8):
                            mm(psum[:, (b + n) * 512:(b + n + 1) * 512], wslV[:, k, col0:col0 + 128],
                               uTV[blk][:, k, :], k == 0, k == 7, [wslB, uTB[blk]], [bankB[b + n]], k == 7)
                    cs = slice(half * 1024, (half + 1) * 1024)
                    if which == "q":
                        act(qTV[:, cs], psum[:, b * 512:(b + 2) * 512], AF.Copy,
                            [bankB[b], bankB[b + 1]], [qTB], scale=0.125)
                    else:
                        act(kTV[0][0:64, cs], psum[0:64, b * 512:(b + 2) * 512], AF.Copy,
                            [bankB[b], bankB[b + 1]], [kTB[0]])
                        act(kTV[1][64:128, cs], psum[64:128, b * 512:(b + 2) * 512], AF.Copy,
                            [bankB[b], bankB[b + 1]], [kTB[1]])

        def proj_z(head):
            hh = head % 2
            for half in range(2):
                b = alloc_banks(2)
                for n in range(2):
                    blk = half * 2 + n
                    for k in range(8):
                        mm(psum[:, (b + n) * 512:(b + n + 1) * 512], wslV[:, k, 512 + hh * 128:512 + (hh + 1) * 128],
                           uTV[blk][:, k, :], k == 0, k == 7, [wslB, uTB[blk]], [bankB[b + n]], k == 7)
                act(szTV2[hh][half], psum[:, b * 512:(b + 2) * 512], AF.Silu, [bankB[b], bankB[b + 1]], [szTB2[hh][half]])

        def v_half_unit(head, tq, part):
            hh, vs = head % 2, head % 2
            b = alloc_banks(1)
            for j, tt in enumerate((2 * part, 2 * part + 1)):
                for k in range(8):
                    mm(psum[:, b * 512 + j * 128:b * 512 + (j + 1) * 128],
                       uTV[tq][:, k, tt * 128:(tt + 1) * 128], wslV[:, k, 256 + hh * 128:256 + (hh + 1) * 128],
                       k == 0, k == 7, [wslB, uTB[tq]], [bankB[b]], (k == 7 and j == 1))
            t0 = tq * 4 + 2 * part
            act(vSV[vs][:, t0:t0 + 2, :],
                psum[:, b * 512:b * 512 + 256].rearrange("p (t n) -> p t n", t=2), AF.Copy,
                [bankB[b]], [vSB[vs]])

        for head in range(16):
            if True:
                hp, hh = divmod(head, 2)
                grp, hin = divmod(head, 4)
                vSc = head % 2
                if head == 0:
                    dma_w(wslV, w1in_d[0, :, :].rearrange("p (k n) -> p k n", k=8), "wsl", [wslB])
                    proj_qk(0)
                    proj_z(0)
                    for tq in range(4):
                        v_half_unit(0, tq, 0)
                        v_half_unit(0, tq, 1)
                if hin == 0:
                    dma_w(woV[0], w1o_d[grp, :, :].rearrange("p (k n) -> p k n", k=4), "wo0", [woB[0]])
                if hh == 1 and hp < 7:
                    dma_w(wslV, w1in_d[hp + 1, :, :].rearrange("p (k n) -> p k n", k=8), "wsl", [wslB])
                filler = [(head + 1, tq, part) for tq in range(4) for part in range(2)] if head < 15 else []

                segs = []
                for sb2 in range(8):
                    for i in (2 * sb2, 2 * sb2 + 1):
                        nseg = i // 4 + 1
                        for sg in reversed(range(nseg)):
                            diag = (sg == nseg - 1)
                            w = ((i % 4) + 1) * 128 if diag else 512
                            segs.append(dict(sb2=sb2, i=i, sg=sg, w=w, diag=diag, hin=hin, hh=hh,
                                             last=(sg == 0 and i == 2 * sb2 + 1)))
                prev = None
                for sd in segs:
                    sd["n"] = seg_i[0]
                    seg_i[0] += 1
                    sd["prev"] = None if sd["diag"] else prev
                    prev = sd

                def stage_a(sd):
                    i, sg, w, sl = sd["i"], sd["sg"], sd["w"], sd["n"] % NOM
                    b = alloc_banks(1)
                    zp = psum[:, b * 512:b * 512 + w]
                    if os.environ.get("KZPAD", "1") == "1":
                        mm(zp, qTV[:, i * 128:(i + 1) * 128], kTV[sd["hh"]][:, sg * 512:sg * 512 + w], True, True,
                           [qTB, kTB[sd["hh"]]], [bankB[b]], True)
                    else:
                        p0 = sd["hh"] * 64
                        mm(zp, qTV[p0:p0 + 64, i * 128:(i + 1) * 128], kTV[sd["hh"]][p0:p0 + 64, sg * 512:sg * 512 + w],
                           True, True, [qTB, kTB[sd["hh"]]], [bankB[b]], True)
                    act(omV[sl][:, 512 - w:512], zp, AF.Sigmoid, [bankB[b]], [omB[sl]], scale=-1.0)

                def stage_m(sd):
                    if sd["diag"]:
                        sl = sd["n"] % NOM
                        S.op("pool", lambda e: e.affine_select(omV[sl][:, 384:512], omV[sl][:, 384:512], [[-1, 128]],
                                                               ALU.is_gt, preg(e, 1.0), base=0, channel_multiplier=1),
                             [omB[sl]], [omB[sl]])

                def stage_b(sd):
                    i, sg, w = sd["i"], sd["sg"], sd["w"]
                    so, sr, sa = sd["n"] % NOM, sd["n"] % NR, sd["n"] % NA
                    lo = 512 - w
                    if sd["prev"] is None:
                        init, rds = 1.0, [omB[so], pzB]
                    else:
                        p = sd["prev"]
                        plo, pr = 512 - p["w"], p["n"] % NR
                        init, rds = RV[pr][:, plo:plo + 1], [omB[so], pzB, RB[pr]]
                    S.op("dve", lambda e: e.tensor_tensor_scan(RV[sr][:, lo:513][:, ::-1], omV[so][:, lo:513][:, ::-1],
                                                               pzV.broadcast_to([128, w + 1]), init, ALU.mult, ALU.add),
                         rds, [RB[sr]])
                    S.op("pool", lambda e: e.tensor_tensor(AV_[sa][:, 0:w], RV[sr][:, lo + 1:513], RV[sr][:, lo:512],
                                                           ALU.subtract), [RB[sr]], [AB[sa]])
                    sl = sa
                    nb = w // 128
                    b = alloc_banks(1)
                    for blk in range(nb):
                        S.op("pe", (lambda blk: lambda e: e.transpose(
                            psum_bf[:, b * 1024 + blk * 128:b * 1024 + (blk + 1) * 128],
                            AV_[sl][:, blk * 128:(blk + 1) * 128], identV))(blk),
                             [AB[sl], identB], [bankB[b]], sig=(blk == nb - 1))
                    ats = sd["sb2"] % 2
                    qo = (i % 2) * 128
                    act(ATV[ats][:, 4 * sg:4 * sg + nb, qo:qo + 128],
                        psum_bf[:, b * 1024:b * 1024 + w].rearrange("p (c n) -> p c n", c=nb), AF.Copy,
                        [bankB[b]], [ATB[ats]])

                def stage_c(sd):
                    sb2 = sd["sb2"]
                    ats, hf = sb2 % 2, sb2 % 2
                    nch = 2 * sb2 + 2
                    for c in range(nch):
                        lo2 = 128 if c == nch - 1 else 0
                        mm(oV[hf][:, lo2:256], vSV[vSc][:, c, :], ATV[ats][:, c, lo2:256],
                           c == 0, c == nch - 1, [vSB[vSc], ATB[ats]], [oB[hf]], c == nch - 1)

                def stage_d(sd):
                    sb2 = sd["sb2"]
                    hf = sb2 % 2
                    half, co = sb2 // 4, (sb2 % 4) * 256
                    y_out = yTV[sd["hin"]][:, sb2 * 256:(sb2 + 1) * 256]
                    zs = sd["hh"]
                    S.op("dve", lambda e: e.tensor_tensor(y_out, oV[hf], szTV2[zs][half][:, co:co + 256], ALU.mult),
                         [oB[hf], szTB2[zs][half]], [yTB[sd["hin"]]])

                last_of = {sd["sb2"]: k for k, sd in enumerate(segs) if sd["last"]}
                c_at = {}
                for sb2, k in last_of.items():
                    c_at.setdefault(min(k + (2 if sb2 == 0 else 3), len(segs) - 1), []).append(sb2)
                done_c, done_d = set(), set()

                def emit_c(sb2):
                    if sb2 >= 2 and (sb2 - 2) not in done_d:
                        stage_d(segs[last_of[sb2 - 2]])
                        done_d.add(sb2 - 2)
                    stage_c(segs[last_of[sb2]])
                    done_c.add(sb2)

                for idx in range(len(segs) + DEPTH):
                    if idx < len(segs):
                        stage_a(segs[idx])
                    if idx >= DEPTH:
                        k = idx - DEPTH
                        stage_b(segs[k])
                        for sb2 in c_at.get(k, []):
                            emit_c(sb2)
                    if idx < len(segs):
                        stage_m(segs[idx])
                    if idx == 40:
                        if hh == 1 and hp < 7:
                            proj_qk(hp + 1)
                        if head < 15:
                            proj_z(head + 1)
                    if filler and idx >= (4 if hh == 0 else 16) and idx % 4 == 0:
                        v_half_unit(*filler.pop(0))
                while filler:
                    v_half_unit(*filler.pop(0))
                for sb2 in range(8):
                    if sb2 not in done_c:
                        emit_c(sb2)
                for sb2 in range(8):
                    if sb2 not in done_d:
                        stage_d(segs[last_of[sb2]])
                        done_d.add(sb2)

                if hin == 3:
                    out_proj([yTV[k] for k in range(4)], list(yTB), 0, list(range(NT)), 0)

        dma_sp(gbV, gfb_d[:, :], "c_gb", (), [gbB])
        rms_stats(list(range(NT)))
        for t in range(NT):
            rmsnorm_tile(t, hV[t], [hB[t]])
            store_h(s, t)

    sems = {}
    for k in list(COMPUTE) + list(S.dcnt.keys()):
        sems[k] = nc.alloc_semaphore(name=f"s_{k}")
    fin = {k: v for k, v in S.dcnt.items() if k.startswith("hs")}
    S.prog["sp"].append((fin, None, None))

    def replay(eng_name, e):
        for waits, emit, sig in S.prog[eng_name]:
            for k, v in waits.items():
                e.wait_ge(sems[k], v)
            if emit is None:
                continue
            ins = emit(e)
            if sig is not None:
                ins.then_inc(sems[sig], 1 if sig in COMPUTE else 16)

    with nc.Block() as block:
        @block.sync
        def _(e):
            replay("sp", e)

        @block.gpsimd
        def _(e):
            replay("pool", e)

        @block.scalar
        def _(e):
            replay("act", e)

        @block.vector
        def _(e):
            replay("dve", e)

        @block.tensor
        def _(e):
            replay("pe", e)
    return nc


_PROGS = {}


def _prog(mode):
    if mode not in _PROGS:
        _PROGS[mode] = build_program(mode)
    return _PROGS[mode]


def _layout_weights(norm_g, pool_w_in, pool_w, pool_scale, pool_w_out, sb_w_in, sb_w_out, norm_f):
    f = np.float32
    c = np.ascontiguousarray
    w0 = np.asarray(pool_w_in, f)[0]
    w0in = c(w0.reshape(8, 128, 8, 512).transpose(2, 1, 0, 3)).reshape(8, 128, 8 * 512)
    wg = np.asarray(pool_w, f)[0]
    w0g = c(wg.reshape(4, 4, 128, 512).transpose(0, 2, 1, 3)).reshape(4, 128, 4 * 512)
    wo0 = np.asarray(pool_w_out, f)[0]
    w0o = c(wo0.reshape(4, 4, 128, 1024).transpose(0, 2, 1, 3)).reshape(4, 128, 4 * 1024)
    w1 = np.asarray(sb_w_in, f)[0]
    q = w1[:, 0:1024].reshape(8, 128, 8, 128)
    kk = w1[:, 1024:2048].reshape(8, 128, 8, 128)
    v = w1[:, 2048:4096].reshape(8, 128, 8, 256)
    z = w1[:, 4096:6144].reshape(8, 128, 8, 256)
    slab = np.concatenate([q, kk, v, z], axis=3)
    w1in = c(slab.transpose(2, 1, 0, 3)).reshape(8, 128, 8 * 768)
    wo1 = np.asarray(sb_w_out, f)[0]
    w1o = c(wo1.reshape(4, 4, 128, 1024).transpose(0, 2, 1, 3)).reshape(4, 128, 4 * 1024)
    ng = np.asarray(norm_g, f)
    return {
        "g0b": c(np.broadcast_to(ng[0], (128, D))), "g1b": c(np.broadcast_to(ng[1], (128, D))),
        "gfb": c(np.broadcast_to(np.asarray(norm_f, f), (128, D))),
        "pscale": c(np.asarray(pool_scale, f)[0].reshape(16, 128).T),
        "w0in": w0in, "w0g": w0g, "w0o": w0o, "w1in": w1in, "w1o": w1o,
    }


FUSED = True


def kernel(x, norm_g, pool_w_in, pool_w, pool_scale, pool_w_out, sb_w_in, sb_w_out, norm_f):
    x = np.ascontiguousarray(np.asarray(x, np.float32))
    wmap = _layout_weights(norm_g, pool_w_in, pool_w, pool_scale, pool_w_out, sb_w_in, sb_w_out, norm_f)
    cores = list(range(NCORES))
    if FUSED:
        in_maps = [dict(wmap, x=x[2 * c:2 * c + 2]) for c in cores]
        res = run_bass_kernel_spmd(_prog("full"), in_maps, core_ids=cores)
        return np.concatenate([r["out"] for r in res.results], axis=0)
    in_maps = [dict(wmap, x=x[2 * c:2 * c + 2]) for c in cores]
    res = run_bass_kernel_spmd(_prog("l0"), in_maps, core_ids=cores)
    in_maps = [dict(wmap, x=np.ascontiguousarray(r["out"])) for r in res.results]
    res = run_bass_kernel_spmd(_prog("l1"), in_maps, core_ids=cores)
    return np.concatenate([r["out"] for r in res.results], axis=0)
```
